# Optimizing a Trainium2 kernel written in Bass

```python
import math
import jax, jax.numpy as jnp
from jax import lax
import numpy as np

D_MODEL = 2048
BATCH = 1
SEQ = 16384
DEPTH = 4

POOL_WIDTH = D_MODEL // 2
POOL_WINDOWS = (2, 4, 8, 16)
POOL_GROUP = POOL_WIDTH // len(POOL_WINDOWS)
DIFF_WIDTH = D_MODEL // 2
DIFF_HEAD_DIM = 64
DIFF_HEADS = DIFF_WIDTH // (2 * DIFF_HEAD_DIM)
AB_WIDTH = POOL_WIDTH + DIFF_WIDTH
AB_IN = POOL_WIDTH + 3 * DIFF_WIDTH + AB_WIDTH
FOURIER_WIDTH = D_MODEL
FOURIER_GROUPS = 4
FOURIER_GROUP = FOURIER_WIDTH // FOURIER_GROUPS
C_IN = 2 * FOURIER_WIDTH
Q_BLOCK = 128
NORM_EPS = 1e-6
N_AB_LAYERS = (DEPTH + 1) // 2
N_C_LAYERS = DEPTH // 2

kernel_name = "hybrid_pool_diffattn_fourier_encoder"


def _alibi_slopes(n_heads):
    return jnp.asarray([2.0 ** (-8.0 * (h + 1) / n_heads) for h in range(n_heads)], dtype=jnp.float32)


def _rms(x, w):
    xf = x.astype(jnp.float32)
    y = xf * lax.rsqrt(jnp.mean(xf * xf, axis=-1, keepdims=True) + NORM_EPS)
    return (y * w.astype(jnp.float32)).astype(x.dtype)


def _modulate(x, norm_w, shift, scale):
    xf = x.astype(jnp.float32)
    y = xf * lax.rsqrt(jnp.mean(xf * xf, axis=-1, keepdims=True) + NORM_EPS) * norm_w.astype(jnp.float32)
    y = y * (1.0 + scale[:, None, :].astype(jnp.float32)) + shift[:, None, :].astype(jnp.float32)
    return y.astype(x.dtype)


def _pool_mix(u, w_pool, pool_scale):
    B, S, _ = u.shape
    t = jnp.arange(S)
    groups = u.reshape(B, S, len(POOL_WINDOWS), POOL_GROUP)
    outs = []
    for g, w in enumerate(POOL_WINDOWS):
        ug = groups[:, :, g].astype(jnp.float32)
        cs = jnp.pad(jnp.cumsum(ug, axis=1), ((0, 0), (1, 0), (0, 0)))
        lo = jnp.clip(t - w // 2, 0, S - 1)
        hi = jnp.clip(t + (w - w // 2) - 1, 0, S - 1)
        win_sum = cs[:, hi + 1] - cs[:, lo]
        cnt = (hi - lo + 1).astype(jnp.float32)
        pooled = (win_sum / cnt[None, :, None] - ug).astype(u.dtype)
        outs.append(jnp.einsum('bsc,cd->bsd', pooled, w_pool[g]))
    return jnp.concatenate(outs, axis=-1) * pool_scale


def _diff_attn(q, k, v, q_norm_w, k_norm_w, lq1, lk1, lq2, lk2, subln_w, lambda_init):
    B, S, H, _, d = q.shape
    q = _rms(q, q_norm_w)
    k = _rms(k, k_norm_w)
    lam = (jnp.exp(jnp.sum(lq1.astype(jnp.float32) * lk1.astype(jnp.float32)))
           - jnp.exp(jnp.sum(lq2.astype(jnp.float32) * lk2.astype(jnp.float32))) + lambda_init)
    slopes = _alibi_slopes(H)
    scale = d ** -0.5
    n_blk = S // Q_BLOCK
    pos = jnp.arange(S)
    q_blocks = q.reshape(B, n_blk, Q_BLOCK, H, 2, d).transpose(1, 0, 2, 3, 4, 5)
    pos_blocks = pos.reshape(n_blk, Q_BLOCK)

    def block(args):
        qb, qpos = args
        s = jnp.einsum('bqhjd,bkhjd->bhjqk', qb, k).astype(jnp.float32) * scale
        dist = jnp.abs(qpos[:, None] - pos[None, :]).astype(jnp.float32)
        s = s - slopes[None, :, None, None, None] * dist[None, None, None]
        p = jax.nn.softmax(s, axis=-1)
        a = (p[:, :, 0] - lam * p[:, :, 1]).astype(v.dtype)
        return jnp.einsum('bhqk,bkhe->bqhe', a, v)

    o = lax.map(block, (q_blocks, pos_blocks))
    o = o.transpose(1, 0, 2, 3, 4).reshape(B, S, H, 2 * d)
    o = _rms(o, subln_w) * (1.0 - lambda_init)
    return o.reshape(B, S, H * 2 * d)


def _fourier_mix(u, w_fourier):
    B, S, _ = u.shape
    ug = u.reshape(B, S, FOURIER_GROUPS, FOURIER_GROUP).astype(jnp.float32)
    f = jnp.fft.fft2(ug, axes=(1, 3), norm='ortho').real.astype(u.dtype)
    return jnp.einsum('bsgc,gcd->bsgd', f, w_fourier).reshape(B, S, FOURIER_WIDTH)


def setup_inputs(seed: int = 0) -> dict:
    key = jax.random.key(seed)
    ks = jax.random.split(key, 20)
    f32 = jnp.float32
    nrm = lambda k, shape, s: jax.random.normal(k, shape, f32) * s
    return {
        "x": nrm(ks[0], (BATCH, SEQ, D_MODEL), 1.0),
        "c": nrm(ks[1], (BATCH, D_MODEL), 1.0),
        "norm_w": 1.0 + nrm(ks[2], (DEPTH, D_MODEL), 0.02),
        "ada_w": nrm(ks[3], (DEPTH, D_MODEL, 3 * D_MODEL), D_MODEL ** -0.5),
        "ada_b": nrm(ks[4], (DEPTH, 3 * D_MODEL), 0.02),
        "w_in_ab": nrm(ks[5], (N_AB_LAYERS, D_MODEL, AB_IN), D_MODEL ** -0.5),
        "w_pool": nrm(ks[6], (N_AB_LAYERS, len(POOL_WINDOWS), POOL_GROUP, POOL_GROUP), POOL_GROUP ** -0.5),
        "pool_scale": 1.0 + nrm(ks[7], (N_AB_LAYERS, POOL_WIDTH), 0.02),
        "q_norm_w": 1.0 + nrm(ks[8], (N_AB_LAYERS, DIFF_HEAD_DIM), 0.02),
        "k_norm_w": 1.0 + nrm(ks[9], (N_AB_LAYERS, DIFF_HEAD_DIM), 0.02),
        "lambda_q1": nrm(ks[10], (N_AB_LAYERS, DIFF_HEAD_DIM), 0.1),
        "lambda_k1": nrm(ks[11], (N_AB_LAYERS, DIFF_HEAD_DIM), 0.1),
        "lambda_q2": nrm(ks[12], (N_AB_LAYERS, DIFF_HEAD_DIM), 0.1),
        "lambda_k2": nrm(ks[13], (N_AB_LAYERS, DIFF_HEAD_DIM), 0.1),
        "subln_w": 1.0 + nrm(ks[14], (N_AB_LAYERS, 2 * DIFF_HEAD_DIM), 0.02),
        "w_out_ab": nrm(ks[15], (N_AB_LAYERS, AB_WIDTH, D_MODEL), AB_WIDTH ** -0.5),
        "w_in_c": nrm(ks[16], (N_C_LAYERS, D_MODEL, C_IN), D_MODEL ** -0.5),
        "w_fourier": nrm(ks[17], (N_C_LAYERS, FOURIER_GROUPS, FOURIER_GROUP, FOURIER_GROUP), FOURIER_GROUP ** -0.5),
        "w_out_c": nrm(ks[18], (N_C_LAYERS, FOURIER_WIDTH, D_MODEL), FOURIER_WIDTH ** -0.5),
    }


def reference(x, c, norm_w, ada_w, ada_b, w_in_ab, w_pool, pool_scale, q_norm_w, k_norm_w,
              lambda_q1, lambda_k1, lambda_q2, lambda_k2, subln_w, w_out_ab, w_in_c, w_fourier, w_out_c):
    B, S, _ = x.shape
    H, d = DIFF_HEADS, DIFF_HEAD_DIM
    c_act = jax.nn.silu(c)
    for i in range(DEPTH):
        mod = c_act @ ada_w[i] + ada_b[i]
        shift, scale, gate = jnp.split(mod, 3, axis=-1)
        h = _modulate(x, norm_w[i], shift, scale)
        j = i // 2
        if i % 2 == 0:
            z = h @ w_in_ab[j]
            o1 = POOL_WIDTH
            o2 = o1 + DIFF_WIDTH
            o3 = o2 + DIFF_WIDTH
            o4 = o3 + DIFF_WIDTH
            u_pool = z[..., :o1]
            q = z[..., o1:o2].reshape(B, S, H, 2, d)
            k = z[..., o2:o3].reshape(B, S, H, 2, d)
            v = z[..., o3:o4].reshape(B, S, H, 2 * d)
            g = z[..., o4:]
            lambda_init = 0.8 - 0.6 * math.exp(-0.3 * i)
            y_a = _pool_mix(u_pool, w_pool[j], pool_scale[j])
            y_b = _diff_attn(q, k, v, q_norm_w[j], k_norm_w[j], lambda_q1[j], lambda_k1[j],
                             lambda_q2[j], lambda_k2[j], subln_w[j], lambda_init)
            y = jnp.concatenate([y_a, y_b], axis=-1) * jax.nn.silu(g)
            out = y @ w_out_ab[j]
        else:
            z = h @ w_in_c[j]
            u, g = z[..., :FOURIER_WIDTH], z[..., FOURIER_WIDTH:]
            y = _fourier_mix(u, w_fourier[j]) * jax.nn.silu(g)
            out = y @ w_out_c[j]
        x = x + gate[:, None, :] * out
    return x
```

```python
import math
from contextlib import ExitStack

import numpy as np
import concourse.bass as bass
import concourse.mybir as mybir
from concourse.bass_utils import run_bass_kernel_spmd

F32 = mybir.dt.float32
BF16 = mybir.dt.bfloat16
ALU = mybir.AluOpType
AF = mybir.ActivationFunctionType
AX = mybir.AxisListType

NCORES = 8
S = 16384
D = 2048
T = 2048
KC = 16
EPS = 1e-6
ENG = ["sync", "scalar", "vector", "gpsimd", "tensor"]
NDMA = 56


class Trk:
    def __init__(self, nc, name):
        self.sem = nc.alloc_semaphore(name=name)
        self.count = 0
        self.name = name


class Res:
    def __init__(self, name):
        self.name = name
        self.w = {}
        self.r = {}


class Tile(Res):
    def __init__(self, name, handle):
        super().__init__(name)
        self.h = handle

    def __getitem__(self, idx):
        return self.h[idx]


class KB:
    def __init__(self, nc):
        self.nc = nc
        self.etrk = {e: Trk(nc, "e_" + e) for e in ["scalar", "vector", "gpsimd", "tensor"]}
        self.seen = {e: {} for e in ENG}
        self.ops = {e: [] for e in ENG}
        self.dpool = [Trk(nc, "d%d" % i) for i in range(NDMA)]
        self.dfree = list(self.dpool)
        self.cc = []
        self.nblk = 0

    def dtrk(self):
        return self.dfree.pop()

    def _need(self, eng, waits, trk, val):
        if val <= 0:
            return
        if eng == "tensor" and trk is self.etrk["tensor"]:
            return
        if self.seen[eng].get(trk, 0) >= val:
            return
        self.seen[eng][trk] = val
        waits.append((trk.sem, val))

    def _deps(self, eng, reads, writes, pwrites=()):
        waits = []
        for r in reads:
            for t, v in r.w.items():
                self._need(eng, waits, t, v)
        for w in writes:
            for t, v in w.w.items():
                self._need(eng, waits, t, v)
            for t, v in w.r.items():
                self._need(eng, waits, t, v)
        for w in pwrites:
            for t, v in w.r.items():
                self._need(eng, waits, t, v)
        return waits

    def _reg(self, v, reads, writes, pwrites=()):
        for r in reads:
            if r.r.get(v[0], 0) < v[1]:
                r.r[v[0]] = v[1]
        for w in writes:
            w.w = {v[0]: v[1]}
            w.r = {}
        for w in pwrites:
            if w.w.get(v[0], 0) < v[1]:
                w.w[v[0]] = v[1]

    def op(self, eng, meth, *args, reads=(), writes=(), pwrites=(), signal=True, **kw):
        waits = self._deps(eng, reads, writes, pwrites)
        trk = self.etrk[eng]
        if signal:
            trk.count += 1
            val = trk.count
            inc = (trk.sem, 1)
        else:
            val = trk.count + 1
            inc = None
        self._reg((trk, val), reads, writes, pwrites)
        self.ops[eng].append((meth, args, kw, waits, inc))

    def dma(self, eng, out, in_, trk, reads=(), writes=(), pwrites=(), **kw):
        waits = self._deps(eng, reads, writes, pwrites)
        self._need(eng, waits, trk, trk.count)
        trk.count += 16
        self._reg((trk, trk.count), reads, writes, pwrites)
        kw = dict(kw)
        kw["out"] = out
        kw["in_"] = in_
        self.ops[eng].append(("dma_start", (), kw, waits, (trk.sem, 16)))

    def allgather(self, src, dst, reads, writes):
        eng = "gpsimd"
        waits = self._deps(eng, reads, writes)
        trk = Trk(self.nc, "cc%d" % len(self.cc))
        self.cc.append(trk)
        trk.count = 1
        self._reg((trk, 1), reads, writes)
        kw = dict(replica_groups=[list(range(NCORES))], ins=[src], outs=[dst])
        self.ops[eng].append(("collective_compute", ("AllGather", ALU.bypass), kw, waits, (trk.sem, None)))

    def flush(self):
        waits = []
        for t in list(self.etrk.values()) + self.dpool + self.cc:
            self._need("sync", waits, t, t.count)
        self.ops["sync"].append((None, (), {}, waits, None))
        self.nblk += 1
        with self.nc.Block("blk%d" % self.nblk) as block:
            for e in ENG:
                ops = self.ops[e]
                if not ops:
                    continue

                def body(eng, ops=ops):
                    ctx = {}
                    for meth, args, kw, waits, inc in ops:
                        for sem, val in waits:
                            eng.wait_ge(sem, val)
                        if meth is None:
                            continue
                        a = [x(eng, ctx) if callable(x) else x for x in args]
                        k = {n: (x(eng, ctx) if callable(x) else x) for n, x in kw.items()}
                        try:
                            ins = getattr(eng, meth)(*a, **k)
                        except Exception:
                            print("FAILED OP", meth, [getattr(x, "shape", x) for x in a], {n: getattr(x, "shape", x) for n, x in k.items()})
                            raise
                        if inc is not None:
                            if inc[1] is None:
                                ins.then_inc(inc[0])
                            else:
                                ins.then_inc(inc[0], inc[1])

                getattr(block, e)(body)
        self.ops = {e: [] for e in ENG}
        self.dfree = list(self.dpool)


_RANK = {}


def _rank(eng, ctx):
    if "rank" not in ctx:
        ctx["rank"] = eng.partition_id()
    return ctx["rank"]


def build(NL=4, debug=False, stop=None):
    nc = bass.Bass("TRN2", target_bir_lowering=False)
    _RANK.clear()
    kb = KB(nc)

    used_inputs = []

    def din(name, shape, dt=F32):
        used_inputs.append(name)
        return nc.dram_tensor(name, list(shape), dt, kind="ExternalInput")

    xT = din("xT", [D, T])
    cvec = din("cvec", [128, 16])
    nw_d = din("nw", [128, 64])
    adaw = din("adaw", [4, D, 768])
    adab = din("adab", [128, 24])
    pscale_d = din("pscale", [128, 16])
    qkw_d = din("qkw", [128, 4])
    lamv_d = din("lamv", [128, 2 * 4 * 64])
    subw_d = din("subw", [128, 2])
    kaug_d = din("kaug", [2, S])
    qaug_d = din("qaug", [4, 512])
    abias_d = din("abias", [128, 257])
    dtab_d = din("dtab", [128, 4 * 512])
    pcoef_d = din("pcoef", [128, 4])
    pratio_d = din("pratio", [128, 16])
    WSHAPE = {"w_in_ab": (D, 6144), "w_pool": (1024, 256), "w_out_ab": (D, D), "w_in_c": (D, 4096),
              "w_fourier": (2048, 512), "w_out_c": (D, D)}
    wfull = {}

    def weight(name, j):
        key = (name, j)
        if key not in wfull:
            rows, cols = WSHAPE[name]
            nm = "%s_%d" % (name, j)
            sh = din(nm, [rows // NCORES, cols])
            bounce = nc.dram_tensor(nm + "_b", [rows // NCORES, cols], F32)
            full = nc.dram_tensor(nm + "_f", [rows, cols], F32)
            Rb, Rf = Res(nm + "_b"), Res(nm + "_f")
            kb.dma("sync", bounce[:, :], sh[:, :], kb.dtrk(), writes=[Rb])
            kb.allgather(bounce.ap().opt(), full.ap().opt(), reads=[Rb], writes=[Rf])
            wfull[key] = (full, Rf)
        return wfull[key]

    yT = nc.dram_tensor("yT", [D, T], F32, kind="ExternalOutput") if stop is None else nc.dram_tensor("yT", [D, T], F32)

    modsend = nc.dram_tensor("modsend", [128, 24], F32)
    modall = nc.dram_tensor("modall", [NCORES * 128, 24], F32)
    xa = nc.dram_tensor("xa", [D, T], F32)
    xb = nc.dram_tensor("xb", [D, T], F32)
    gbuf = nc.dram_tensor("gbuf", [D, T], BF16)
    send1 = nc.dram_tensor("send1", [4096, 2048], BF16)
    recv1 = nc.dram_tensor("recv1", [NCORES * 4096, 2048], BF16)
    loc1 = nc.dram_tensor("loc1", [512, S], BF16)
    send2 = nc.dram_tensor("send2", [256, S], BF16)
    recv2 = nc.dram_tensor("recv2", [NCORES * 256, S], BF16)
    R_xT, R_xa, R_xb, R_y = Res("xT"), Res("xa"), Res("xb"), Res("yT")
    R_gbuf, R_send1, R_recv1, R_loc1 = Res("gbuf"), Res("send1"), Res("recv1"), Res("loc1")
    R_send2, R_recv2 = Res("send2"), Res("recv2")
    R_modsend, R_modall = Res("modsend"), Res("modall")
    if NL > 1:
        send3 = nc.dram_tensor("send3", [4096, 2048], BF16)
        recv3 = nc.dram_tensor("recv3", [NCORES * 4096, 2048], BF16)
        send4 = nc.dram_tensor("send4", [128, 32768], BF16)
        recv4 = nc.dram_tensor("recv4", [NCORES * 128, 32768], BF16)
        R_send3, R_recv3, R_send4, R_recv4 = Res("send3"), Res("recv3"), Res("send4"), Res("recv4")
        loc3 = nc.dram_tensor("loc3", [4096, 2048], BF16)
        loc4 = nc.dram_tensor("loc4", [128, 32768], BF16)
        R_loc3, R_loc4 = Res("loc3"), Res("loc4")
        cs512_d = din("cs512", [512, 1024])
        f1_d = din("f1c", [128, 256])
        f2_d = din("f2c", [128, 256])
        tw_d = din("twc", [128, 1024])

    with ExitStack() as glob:
        uid = [0]

        def sb(es, name, shape, dt):
            uid[0] += 1
            name = "s%d_%s" % (uid[0], name)
            return Tile(name, es.enter_context(nc.sbuf_tensor(name, list(shape), dt)))

        def ps(es, name, shape):
            uid[0] += 1
            name = "p%d_%s" % (uid[0], name)
            return Tile(name, es.enter_context(nc.psum_tensor(name, list(shape), F32)))

        ones = sb(glob, "ones", [128, 128], BF16)
        bones = sb(glob, "bones", [128, 128], BF16)
        ones32 = sb(glob, "ones32", [128, 128], F32)
        modT = sb(glob, "modT", [128, NCORES * 24], F32)
        nwt = sb(glob, "nwt", [128, 64], F32)
        acoef = sb(glob, "acoef", [128, 16], F32)
        pscale = sb(glob, "pscale_t", [128, 16], F32)
        qkw = sb(glob, "qkw_t", [128, 4], F32)
        subw = sb(glob, "subw_t", [128, 2], F32)
        lamt = sb(glob, "lamt", [128, 8], F32)
        pcoef = sb(glob, "pcoef_t", [128, 4], F32)
        pratio = sb(glob, "pratio_t", [128, 16], F32)

        def mcol(i, ci):
            c = (ci // 6) * 24 + i * 6 + (ci % 6)
            return modT[:, c:c + 1]

        with ExitStack() as es:
            cv = sb(es, "cv", [128, 16], F32)
            cact = sb(es, "cact", [128, 16], F32)
            adab_t = sb(es, "adab_t", [128, 24], F32)
            modsl = sb(es, "modsl", [128, 24], F32)
            aw = [sb(es, "aw%d" % b, [128, 16, 768], F32) for b in range(2)]
            lamv = sb(es, "lamv", [128, 512], F32)
            lprod = sb(es, "lprod", [128, 512], F32)
            lsum = sb(es, "lsum", [128, 8], F32)
            mps = ps(es, "mps", [128, 512])
            trk = [kb.dtrk() for _ in range(12)]
            kb.op("vector", "memset", ones[:], 1.0, writes=[ones])
            kb.op("vector", "memset", ones32[:], 1.0, writes=[ones32])
            kb.op("vector", "memset", bones[:], 0.0, writes=[bones])
            kb.op("vector", "memset", bones[0:64, 0:64], 1.0, writes=[bones])
            kb.op("vector", "memset", bones[64:128, 64:128], 1.0, writes=[bones])
            kb.dma("sync", cv[:], cvec[:, :], trk[0], writes=[cv])
            kb.dma("sync", adab_t[:], adab[:, :], trk[1], writes=[adab_t])
            kb.dma("sync", nwt[:], nw_d[:, :], trk[2], writes=[nwt])
            kb.dma("sync", pscale[:], pscale_d[:, :], trk[3], writes=[pscale])
            kb.dma("sync", qkw[:], qkw_d[:, :], trk[4], pwrites=[qkw])
            kb.dma("sync", subw[:], subw_d[:, :], trk[5], writes=[subw])
            kb.dma("sync", lamv[:], lamv_d[:, :], trk[6], writes=[lamv])
            kb.dma("sync", pcoef[:], pcoef_d[:, :], trk[7], writes=[pcoef])
            kb.dma("sync", pratio[:], pratio_d[:, :], trk[8], writes=[pratio])
            kb.op("scalar", "activation", cact[:], cv[:], AF.Silu, reads=[cv], writes=[cact])
            for i in range(4):
                a = aw[i % 2]
                kb.dma("sync", a[:], adaw[i].rearrange("(k p) c -> p k c", p=128), trk[9 + i % 2], writes=[a])
                for b in range(6):
                    col = i * 6 + b
                    for k in range(KC):
                        kb.op("tensor", "matmul", mps[:, col:col + 1], a[:, k, b * 128:(b + 1) * 128], cact[:, k:k + 1],
                              start=(k == 0), stop=(k == KC - 1), reads=[a, cact], writes=[mps], signal=(k == KC - 1))
            kb.op("vector", "tensor_tensor", modsl[:], mps[:, 0:24], adab_t[:], ALU.add, reads=[mps, adab_t], writes=[modsl])
            kb.dma("sync", modsend[:, :], modsl[:], trk[11], reads=[modsl], writes=[R_modsend])
            kb.allgather(modsend.ap().opt(), modall.ap().opt(), reads=[R_modsend], writes=[R_modall])
            kb.dma("gpsimd", modT[:].rearrange("p (r c) -> p r c", c=24), modall.ap().rearrange("(r p) c -> p r c", p=128),
                   trk[0], reads=[R_modall], writes=[modT])
            kb.op("vector", "tensor_scalar", qkw[:, 0:1], qkw[:, 0:1], 0.125, None, ALU.mult, reads=[qkw], pwrites=[qkw])
            kb.op("vector", "tensor_scalar", qkw[:, 2:3], qkw[:, 2:3], 0.125, None, ALU.mult, reads=[qkw], pwrites=[qkw])
            for j in range(2):
                for t in range(2):
                    base = j * 256 + t * 128
                    kb.op("vector", "tensor_tensor", lprod[:, base:base + 64], lamv[:, base:base + 64],
                          lamv[:, base + 64:base + 128], ALU.mult, reads=[lamv], pwrites=[lprod])
                    kb.op("vector", "tensor_reduce", lsum[:, 2 * j + t:2 * j + t + 1], lprod[:, base:base + 64], AX.X, ALU.add,
                          reads=[lprod], pwrites=[lsum])
            kb.op("scalar", "activation", lsum[:, 0:4], lsum[:, 0:4], AF.Exp, reads=[lsum], pwrites=[lsum])
            for j in range(2):
                li = 0.8 - 0.6 * math.exp(-0.3 * (2 * j))
                kb.op("vector", "tensor_tensor", lamt[:, 2 * j:2 * j + 1], lsum[:, 2 * j + 1:2 * j + 2], lsum[:, 2 * j:2 * j + 1],
                      ALU.subtract, reads=[lsum], pwrites=[lamt])
                kb.op("vector", "tensor_scalar", lamt[:, 2 * j:2 * j + 1], lamt[:, 2 * j:2 * j + 1], -li, None, ALU.add,
                      reads=[lamt], pwrites=[lamt])
                kb.op("vector", "tensor_scalar", lamt[:, 2 * j + 1:2 * j + 2], subw[:, j:j + 1], 1.0 - li, None, ALU.mult,
                      reads=[subw], pwrites=[lamt])
            for li_ in range(NL):
                if li_ % 2 == 0:
                    weight("w_in_ab", li_ // 2)
                    if stop is None or stop.startswith("C"):
                        weight("w_pool", li_ // 2)
                        weight("w_out_ab", li_ // 2)
                else:
                    weight("w_in_c", li_ // 2)
                    if stop is None:
                        weight("w_fourier", li_ // 2)
                        weight("w_out_c", li_ // 2)
            kb.flush()

        def stage_in(i, xsrc, R_src, hT):
            with ExitStack() as es:
                xs = [sb(es, "xs%d" % b, [128, T], F32) for b in range(2)]
                sq = [sb(es, "sq%d" % b, [128, T], BF16) for b in range(2)]
                rstd = sb(es, "rstd", [128, T], F32)
                tmp = [sb(es, "tmpx%d" % b, [128, T], F32) for b in range(2)]
                ssps = ps(es, "ssps", [128, T])
                tx = [kb.dtrk() for _ in range(2)]
                for k in range(KC):
                    kb.op("vector", "scalar_tensor_tensor", acoef[:, k:k + 1], mcol(i, 16 + k), 1.0, nwt[:, i * 16 + k:i * 16 + k + 1],
                          ALU.add, ALU.mult, reads=[modT, nwt], pwrites=[acoef])
                for k in range(KC):
                    b = k % 2
                    kb.dma("sync", xs[b][:], xsrc[k * 128:(k + 1) * 128, :], tx[b], reads=[R_src], writes=[xs[b]])
                    kb.op("scalar", "activation", sq[b][:], xs[b][:], AF.Square, reads=[xs[b]], writes=[sq[b]])
                    for t in range(4):
                        kb.op("tensor", "matmul", ssps[:, t * 512:(t + 1) * 512], ones[:], sq[b][:, t * 512:(t + 1) * 512],
                              start=(k == 0), stop=(k == KC - 1), reads=[sq[b], ones], writes=[ssps], signal=(t == 3))
                kb.op("scalar", "activation", rstd[:], ssps[:], AF.Sqrt, bias=EPS, scale=1.0 / D, reads=[ssps], writes=[rstd])
                kb.op("vector", "reciprocal", rstd[:], rstd[:], reads=[rstd], writes=[rstd])
                for k in range(KC):
                    b = k % 2
                    kb.dma("sync", xs[b][:], xsrc[k * 128:(k + 1) * 128, :], tx[b], reads=[R_src], writes=[xs[b]])
                    kb.op("vector", "tensor_tensor", tmp[b][:], xs[b][:], rstd[:], ALU.mult, reads=[xs[b], rstd], writes=[tmp[b]])
                    kb.op("vector", "tensor_scalar", hT[:, k, :], tmp[b][:], acoef[:, k:k + 1], mcol(i, k), ALU.mult, ALU.add,
                          reads=[tmp[b], acoef, modT], pwrites=[hT])
                kb.flush()

        def layer_ab(i, xsrc, R_src, xdst, R_dst):
            j = i // 2
            s1v = send1.ap().rearrange("(j t p) c -> j t p c", j=8, t=4)

            with ExitStack() as outer:
                hT = sb(outer, "hT", [128, KC, T], BF16)
                stage_in(i, xsrc, R_src, hT)
                with ExitStack() as es:
                    wb = [sb(es, "wb%d" % b, [128, KC, 256], BF16) for b in range(2)]
                    ob = [sb(es, "ob%d" % b, [128, 1024], BF16) for b in range(3)]
                    sqq = [sb(es, "sqq%d" % b, [128, 1024], BF16) for b in range(2)]
                    rq = [sb(es, "rq%d" % b, [128, 1024], F32) for b in range(2)]
                    vb = [sb(es, "vb%d" % b, [128, 256], BF16) for b in range(2)]
                    zps = [ps(es, "zps%d" % b, [128, 1024]) for b in range(2)]
                    ssq = [ps(es, "ssq%d" % b, [128, 1024]) for b in range(2)]
                    twb = [kb.dtrk() for _ in range(2)]
                    tob = [kb.dtrk() for _ in range(3)]
                    tvb = [kb.dtrk() for _ in range(4)]
                    wfd, R_w = weight("w_in_ab", j)
                    wsrc = wfd.ap().rearrange("(k p) c -> p k c", p=128)
                    zc = 0
                    oc = 0
                    vc = 0
                    for wp in range(24):
                        w = wb[wp % 2]
                        kb.dma("gpsimd", w[:], wsrc[:, :, wp * 256:(wp + 1) * 256], twb[wp % 2], reads=[R_w], writes=[w])
                        if 12 <= wp < 16:
                            for tt in range(16):
                                zp = zps[zc % 2]
                                zc += 1
                                for k in range(KC):
                                    kb.op("tensor", "matmul", zp[:, 0:256], hT[:, k, tt * 128:(tt + 1) * 128], w[:, k, :],
                                          start=(k == 0), stop=(k == KC - 1), reads=[hT, w], writes=[zp], signal=(k == KC - 1))
                                v = vb[vc % 2]
                                kb.op("scalar", "activation", v[:], zp[:, 0:256], AF.Copy, reads=[zp], writes=[v])
                                for hl in range(2):
                                    hd = 2 * (wp - 12) + hl
                                    dst = s1v[hd, 3].rearrange("p (t e) -> p t e", e=128)[:, tt, :]
                                    kb.dma("sync", dst, v[:, hl * 128:(hl + 1) * 128], tvb[(2 * vc + hl) % 4], reads=[v], pwrites=[R_send1])
                                vc += 1
                            continue
                        for c2 in range(2):
                            cb = 2 * wp + c2
                            for th in range(2):
                                zp = zps[zc % 2]
                                zc += 1
                                for k in range(KC):
                                    for t in range(2):
                                        kb.op("tensor", "matmul", zp[:, t * 512:(t + 1) * 512], w[:, k, c2 * 128:(c2 + 1) * 128],
                                              hT[:, k, th * 1024 + t * 512: th * 1024 + (t + 1) * 512],
                                              start=(k == 0), stop=(k == KC - 1), reads=[hT, w], writes=[zp],
                                              signal=(k == KC - 1 and t == 1))
                                o = ob[oc % 3]
                                to = tob[oc % 3]
                                oc += 1
                                if cb < 8 or cb >= 32:
                                    if cb < 8:
                                        kb.op("scalar", "activation", o[:], zp[:], AF.Copy, reads=[zp], writes=[o])
                                        dst = s1v[cb, 2][:, th * 1024:(th + 1) * 1024]
                                        kb.dma("sync", dst, o[:], to, reads=[o], pwrites=[R_send1])
                                    else:
                                        kb.op("scalar", "activation", o[:], zp[:], AF.Silu, reads=[zp], writes=[o])
                                        gb = cb - 32
                                        dst = gbuf[gb * 128:(gb + 1) * 128, th * 1024:(th + 1) * 1024]
                                        kb.dma("sync", dst, o[:], to, reads=[o], pwrites=[R_gbuf])
                                else:
                                    isq = cb < 16
                                    hd = cb - 8 if isq else cb - 16
                                    sq_ = sqq[oc % 2]
                                    r_ = rq[oc % 2]
                                    sp = ssq[oc % 2]
                                    kb.op("scalar", "activation", sq_[:], zp[:], AF.Square, reads=[zp], writes=[sq_])
                                    for t in range(2):
                                        kb.op("tensor", "matmul", sp[:, t * 512:(t + 1) * 512], bones[:], sq_[:, t * 512:(t + 1) * 512],
                                              start=True, stop=True, reads=[bones, sq_], writes=[sp], signal=(t == 1))
                                    kb.op("scalar", "activation", r_[:], sp[:], AF.Sqrt, bias=EPS, scale=1.0 / 64, reads=[sp], writes=[r_])
                                    kb.op("vector", "reciprocal", r_[:], r_[:], reads=[r_], writes=[r_])
                                    wcol = 2 * j + (0 if isq else 1)
                                    kb.op("vector", "scalar_tensor_tensor", o[:], zp[:], qkw[:, wcol:wcol + 1], r_[:], ALU.mult, ALU.mult,
                                          reads=[zp, qkw, r_], writes=[o])
                                    dst = s1v[hd, 0 if isq else 1][:, th * 1024:(th + 1) * 1024]
                                    kb.dma("sync", dst, o[:], to, reads=[o], pwrites=[R_send1])
                    kb.flush()

            if stop == "A1":
                return True
            kb.allgather(send1.ap().opt(), recv1.ap().opt(), reads=[R_send1], writes=[R_recv1])
            r1all = recv1.ap().rearrange("(s j r) c -> j r s c", s=8, j=8)
            s2v = send2.ap().rearrange("(t p) c -> t p c", t=2)

            with ExitStack() as es:
                up = sb(es, "up", [128, S + 32], BF16)
                P_ = sb(es, "poolP", [128, 4096 + 16], F32)
                Q_ = sb(es, "poolQ", [128, 4096 + 16], F32)
                A_ = sb(es, "poolA", [128, 4096 + 16], F32)
                po = [sb(es, "po%d" % b, [128, 4096], BF16) for b in range(2)]
                tu = kb.dtrk()
                tq = kb.dtrk()
                tpo = [kb.dtrk() for _ in range(2)]
                kb.op("vector", "memset", up[:, 0:16], 0.0, pwrites=[up])
                kb.op("vector", "memset", up[:, 16 + S:32 + S], 0.0, pwrites=[up])
                kb.dma("gpsimd", loc1.ap().rearrange("r (s c) -> r s c", s=8),
                       lambda e, c: r1all[_rank(e, c)], tq, reads=[R_recv1], writes=[R_loc1])
                kb.dma("sync", up[:, 16:16 + S], loc1[256:384, :], tu, reads=[R_loc1], pwrites=[up])
                W = 4096 + 16
                for c in range(4):
                    base = 16 + 4096 * c - 8
                    kb.op("vector", "tensor_tensor", P_[:, 0:W], up[:, base:base + W], up[:, base - 1:base - 1 + W], ALU.add,
                          reads=[up], writes=[P_])
                    kb.op("vector", "tensor_scalar", A_[:, 0:W], P_[:, 0:W], pcoef[:, 0:1], None, ALU.mult, reads=[P_, pcoef], writes=[A_])
                    kb.op("vector", "tensor_tensor", Q_[:, 1:W - 1], P_[:, 2:W], P_[:, 0:W - 2], ALU.add, reads=[P_], writes=[Q_])
                    kb.op("vector", "scalar_tensor_tensor", A_[:, 1:W - 1], Q_[:, 1:W - 1], pcoef[:, 1:2], A_[:, 1:W - 1], ALU.mult, ALU.add,
                          reads=[Q_, pcoef, A_], writes=[A_])
                    kb.op("vector", "tensor_tensor", P_[:, 3:W - 3], Q_[:, 5:W - 1], Q_[:, 1:W - 5], ALU.add, reads=[Q_], writes=[P_])
                    kb.op("vector", "scalar_tensor_tensor", A_[:, 3:W - 3], P_[:, 3:W - 3], pcoef[:, 2:3], A_[:, 3:W - 3], ALU.mult, ALU.add,
                          reads=[P_, pcoef, A_], writes=[A_])
                    kb.op("vector", "tensor_tensor", Q_[:, 7:W - 7], P_[:, 11:W - 3], P_[:, 3:W - 11], ALU.add, reads=[P_], writes=[Q_])
                    kb.op("vector", "scalar_tensor_tensor", A_[:, 7:W - 7], Q_[:, 7:W - 7], pcoef[:, 3:4], A_[:, 7:W - 7], ALU.mult, ALU.add,
                          reads=[Q_, pcoef, A_], writes=[A_])
                    if c == 0:
                        kb.op("vector", "tensor_tensor", A_[:, 8:16], A_[:, 8:16], pratio[:, 0:8], ALU.mult, reads=[A_, pratio], writes=[A_])
                    if c == 3:
                        kb.op("vector", "tensor_tensor", A_[:, 8 + 4088:8 + 4096], A_[:, 8 + 4088:8 + 4096], pratio[:, 8:16], ALU.mult,
                              reads=[A_, pratio], writes=[A_])
                    o = po[c % 2]
                    kb.op("vector", "tensor_tensor", o[:], A_[:, 8:8 + 4096], up[:, 16 + 4096 * c:16 + 4096 * (c + 1)], ALU.subtract,
                          reads=[A_, up], writes=[o])
                    kb.dma("sync", s2v[0][:, 4096 * c:4096 * (c + 1)], o[:], tpo[c % 2], reads=[o], pwrites=[R_send2])
                kb.flush()

            if stop == "A2a":
                return True
            with ExitStack() as es:
                KT = [sb(es, "KT%d" % m, [66, S], BF16) for m in range(2)]
                V = sb(es, "V", [128, 128, 128], BF16)
                QS = [[[sb(es, "Q%s%d%d" % (v, m, b), [66, 512], BF16) for b in range(2)] for m in range(2)] for v in "LR"]
                abias = sb(es, "abias", [128, 257], F32)
                dtab = sb(es, "dtab", [128, 4, 512], F32)
                Pb = [sb(es, "Pb%d" % b, [128, 1024], BF16) for b in range(4)]
                scs = [sb(es, "scs%d" % b, [128, 1024], F32) for b in range(2)]
                accA = [sb(es, "accA%d" % b, [128, 1024], F32) for b in range(2)]
                acch = sb(es, "acch", [128, 1024], BF16)
                accl = sb(es, "accl", [128, 1024], BF16)
                r1 = sb(es, "r1", [128, 512], F32)
                r2 = sb(es, "r2", [128, 512], F32)
                o1 = sb(es, "o1", [128, 512], F32)
                o2 = sb(es, "o2", [128, 512], F32)
                od = sb(es, "od", [128, 512], F32)
                osq = sb(es, "osq", [128, 512], BF16)
                rr = sb(es, "rr", [128, 512], F32)
                on = [sb(es, "on%d" % b, [128, 512], BF16) for b in range(2)]
                scp = [ps(es, "scp%d" % b, [128, 1024]) for b in range(2)]
                Op = [ps(es, "Op%d" % m, [128, 512]) for m in range(2)]
                Lp = [ps(es, "Lp%d" % m, [128, 512]) for m in range(2)]
                tk = [kb.dtrk() for _ in range(8)]
                tqs = [[[kb.dtrk() for b in range(2)] for m in range(2)] for v in range(2)]
                ton = [kb.dtrk() for _ in range(2)]
                for m in range(2):
                    kb.dma("sync", KT[m][0:64, :], loc1[128 + 64 * m:128 + 64 * (m + 1), :], tk[m], reads=[R_loc1], pwrites=[KT[m]])
                    kb.dma("gpsimd", KT[m][64:66, :].rearrange("p (a c) -> p a c", c=2048), kaug_d.ap().rearrange("p (a c) -> p a c", c=2048), tk[2 + m], pwrites=[KT[m]])
                kb.dma("sync", V[:].rearrange("p a e -> p (a e)"), loc1[384:512, :], tk[4], reads=[R_loc1], writes=[V])
                kb.dma("sync", abias[:], abias_d[:, :], tk[5], writes=[abias])
                kb.dma("sync", dtab[:].rearrange("p r q -> p (r q)"), dtab_d[:, :], tk[6], writes=[dtab])
                for vi in range(2):
                    for m in range(2):
                        for b in range(2):
                            kb.dma("gpsimd", QS[vi][m][b][64:66, :], qaug_d[2 * vi:2 * vi + 2, :], tqs[vi][m][b], pwrites=[QS[vi][m][b]])

                items = [(qb, kbk) for qb in range(32) for kbk in range(128)]

                def load_q(qb):
                    b = qb % 2
                    for vi in range(2):
                        for m in range(2):
                            q = QS[vi][m][b]
                            kb.dma("sync", q[0:64, :], loc1[64 * m:64 * (m + 1), qb * 512:(qb + 1) * 512], tqs[vi][m][b],
                                   reads=[R_loc1], pwrites=[q])

                def kind(qb, kbk):
                    if kbk < 4 * qb:
                        return 0
                    if kbk < 4 * qb + 4:
                        return 2
                    return 1

                def emit_S(idx):
                    qb, kbk = items[idx]
                    if kbk == 32 and qb + 1 < 32:
                        load_q(qb + 1)
                    kd = kind(qb, kbk)
                    sp = scp[idx % 2]
                    rows = 64 if kd == 2 else 66
                    for m in range(2):
                        q = QS[0 if kd != 1 else 1][m][qb % 2]
                        kb.op("tensor", "matmul", sp[:, m * 512:(m + 1) * 512], KT[m][0:rows, kbk * 128:(kbk + 1) * 128], q[0:rows, :],
                              start=True, stop=True, reads=[KT[m], q], writes=[sp], signal=(m == 1))
                    p = Pb[idx % 4]
                    if kd == 2:
                        r = kbk - 4 * qb
                        sc = scs[idx % 2]
                        for m in range(2):
                            kb.op("vector", "tensor_tensor", sc[:, m * 512:(m + 1) * 512], sp[:, m * 512:(m + 1) * 512], dtab[:, r, :], ALU.add,
                                  reads=[sp, dtab], pwrites=[sc])
                        kb.op("scalar", "activation", p[:], sc[:], AF.Exp, reads=[sc], writes=[p])
                    else:
                        if kd == 0:
                            n = 4 * qb - kbk
                            col = n
                        else:
                            n = kbk - 4 * qb
                            col = 129 + n
                        kb.op("scalar", "activation", p[:], sp[:], AF.Exp, bias=abias[:, col:col + 1], reads=[sp, abias], writes=[p])

                def emit_AV(idx):
                    qb, kbk = items[idx]
                    p = Pb[idx % 4]
                    for m in range(2):
                        kb.op("tensor", "matmul", Op[m][:], V[:, kbk, :], p[:, m * 512:(m + 1) * 512], start=(kbk == 0), stop=(kbk == 127),
                              reads=[V, p], writes=[Op[m]], signal=(m == 1))
                    aA = accA[qb % 2]
                    if kbk == 0:
                        kb.op("vector", "tensor_copy", aA[:], p[:], reads=[p], writes=[aA])
                    else:
                        kb.op("vector", "tensor_tensor", aA[:], aA[:], p[:], ALU.add, reads=[p, aA], writes=[aA])
                    if kbk == 127:
                        kb.op("vector", "tensor_copy", acch[:], aA[:], reads=[aA], writes=[acch])
                        kb.op("vector", "tensor_tensor", accl[:], aA[:], acch[:], ALU.subtract, reads=[aA, acch], writes=[accl])
                        for m in range(2):
                            kb.op("tensor", "matmul", Lp[m][:], ones[:], acch[:, m * 512:(m + 1) * 512], start=True, stop=False,
                                  reads=[ones, acch], writes=[Lp[m]], signal=False)
                            kb.op("tensor", "matmul", Lp[m][:], ones[:], accl[:, m * 512:(m + 1) * 512], start=False, stop=True,
                                  reads=[ones, accl], writes=[Lp[m]], signal=True)
                        post(qb)

                def post(qb):
                    kb.op("vector", "reciprocal", r1[:], Lp[0][:], reads=[Lp[0]], writes=[r1])
                    kb.op("vector", "reciprocal", r2[:], Lp[1][:], reads=[Lp[1]], writes=[r2])
                    kb.op("vector", "tensor_tensor", o1[:], Op[0][:], r1[:], ALU.mult, reads=[Op[0], r1], writes=[o1])
                    kb.op("vector", "tensor_tensor", o2[:], Op[1][:], r2[:], ALU.mult, reads=[Op[1], r2], writes=[o2])
                    kb.op("vector", "scalar_tensor_tensor", od[:], o2[:], lamt[:, 2 * j:2 * j + 1], o1[:], ALU.mult, ALU.add,
                          reads=[o2, lamt, o1], writes=[od])
                    kb.op("scalar", "activation", osq[:], od[:], AF.Square, reads=[od], writes=[osq])
                    kb.op("tensor", "matmul", Lp[0][:], ones[:], osq[:], start=True, stop=True, reads=[ones, osq], writes=[Lp[0]])
                    kb.op("scalar", "activation", rr[:], Lp[0][:], AF.Sqrt, bias=EPS, scale=1.0 / 128, reads=[Lp[0]], writes=[rr])
                    kb.op("vector", "reciprocal", rr[:], rr[:], reads=[rr], writes=[rr])
                    o = on[qb % 2]
                    kb.op("vector", "scalar_tensor_tensor", o[:], od[:], lamt[:, 2 * j + 1:2 * j + 2], rr[:], ALU.mult, ALU.mult,
                          reads=[od, lamt, rr], writes=[o])
                    kb.dma("sync", s2v[1][:, qb * 512:(qb + 1) * 512], o[:], ton[qb % 2], reads=[o], pwrites=[R_send2])

                load_q(0)
                emit_S(0)
                for idx in range(len(items)):
                    if idx + 1 < len(items):
                        emit_S(idx + 1)
                    emit_AV(idx)
                kb.flush()

            if stop == "A2b":
                return True
            kb.allgather(send2.ap().opt(), recv2.ap().opt(), reads=[R_send2], writes=[R_recv2])
            r2v = recv2.ap().rearrange("(jt p) (r c) -> r p jt c", jt=16, r=8)

            with ExitStack() as es:
                yt = sb(es, "yt", [128, 16, T], BF16)
                sg = [sb(es, "sg%d" % b, [128, T], BF16) for b in range(2)]
                wpl = sb(es, "wpl", [128, 8, 256], BF16)
                wo = [sb(es, "wo%d" % b, [128, KC, 256], BF16) for b in range(2)]
                xin = [sb(es, "xin%d" % b, [128, T], F32) for b in range(2)]
                xo = [sb(es, "xo%d" % b, [128, T], F32) for b in range(2)]
                pp = [ps(es, "pp%d" % b, [128, T]) for b in range(2)]
                ty = [kb.dtrk() for _ in range(2)]
                tsg = [kb.dtrk() for _ in range(2)]
                twp = kb.dtrk()
                two = [kb.dtrk() for _ in range(2)]
                txi = [kb.dtrk() for _ in range(2)]
                txo = [kb.dtrk() for _ in range(2)]
                kb.dma("gpsimd", yt[:], lambda e, c: r2v[_rank(e, c)], ty[0], reads=[R_recv2], writes=[yt])
                wfd, R_w = weight("w_pool", j)
                kb.dma("gpsimd", wpl[:], wfd.ap().rearrange("(g k p) d -> p (g k) d", g=4, p=128), twp, reads=[R_w], writes=[wpl])
                for blk in range(8):
                    g = blk // 2
                    dh = blk % 2
                    pz = pp[blk % 2]
                    s = sg[blk % 2]
                    kb.dma("sync", s[:], gbuf[blk * 128:(blk + 1) * 128, :], tsg[blk % 2], reads=[R_gbuf], writes=[s])
                    for kk in range(2):
                        for t in range(4):
                            kb.op("tensor", "matmul", pz[:, t * 512:(t + 1) * 512], wpl[:, 2 * g + kk, dh * 128:(dh + 1) * 128],
                                  yt[:, 2 * (2 * g + kk), t * 512:(t + 1) * 512], start=(kk == 0), stop=(kk == 1), reads=[wpl, yt], writes=[pz],
                                  signal=(kk == 1 and t == 3))
                    if dh == 0:
                        kb.op("vector", "scalar_tensor_tensor", s[:], pz[:], pscale[:, j * 8 + blk:j * 8 + blk + 1], s[:], ALU.mult, ALU.mult,
                              reads=[pz, pscale, s], writes=[s])
                        keep = s
                    else:
                        kb.op("vector", "scalar_tensor_tensor", yt[:, 2 * blk, :], pz[:], pscale[:, j * 8 + blk:j * 8 + blk + 1], s[:], ALU.mult, ALU.mult,
                              reads=[pz, pscale, s], pwrites=[yt])
                        kb.op("vector", "tensor_copy", yt[:, 2 * (blk - 1), :], keep[:], reads=[keep], pwrites=[yt])
                for hd in range(8):
                    blk = 8 + hd
                    s = sg[blk % 2]
                    kb.dma("sync", s[:], gbuf[blk * 128:(blk + 1) * 128, :], tsg[blk % 2], reads=[R_gbuf], writes=[s])
                    kb.op("vector", "tensor_tensor", yt[:, 2 * hd + 1, :], yt[:, 2 * hd + 1, :], s[:], ALU.mult, reads=[yt, s], pwrites=[yt])
                wfd, R_w = weight("w_out_ab", j)
                wsrc = wfd.ap().rearrange("(k p) c -> p k c", p=128)
                for wp in range(8):
                    w = wo[wp % 2]
                    kb.dma("gpsimd", w[:], wsrc[:, :, wp * 256:(wp + 1) * 256], two[wp % 2], reads=[R_w], writes=[w])
                    for c2 in range(2):
                        db = 2 * wp + c2
                        pz = pp[db % 2]
                        xi = xin[db % 2]
                        x_o = xo[db % 2]
                        kb.dma("sync", xi[:], xsrc[db * 128:(db + 1) * 128, :], txi[db % 2], reads=[R_src], writes=[xi])
                        for k in range(KC):
                            for t in range(4):
                                kb.op("tensor", "matmul", pz[:, t * 512:(t + 1) * 512], w[:, k, c2 * 128:(c2 + 1) * 128],
                                      yt[:, 2 * (k % 8) + (k // 8), t * 512:(t + 1) * 512], start=(k == 0), stop=(k == KC - 1), reads=[w, yt], writes=[pz],
                                      signal=(k == KC - 1 and t == 3))
                        kb.op("vector", "scalar_tensor_tensor", x_o[:], pz[:], mcol(i, 32 + db), xi[:], ALU.mult, ALU.add,
                              reads=[pz, modT, xi], writes=[x_o])
                        kb.dma("sync", xdst[db * 128:(db + 1) * 128, :], x_o[:], txo[db % 2], reads=[x_o], pwrites=[R_dst])
                kb.flush()


        def layer_c(i, xsrc, R_src, xdst, R_dst):
            j = i // 2
            s3v = send3.ap().rearrange("(j r m) c -> j r m c", j=8, r=2)
            with ExitStack() as outer:
                uT = sb(outer, "uT", [128, KC, T], BF16)
                with ExitStack() as mid:
                    hT = sb(mid, "hT", [128, KC, T], BF16)
                    stage_in(i, xsrc, R_src, hT)
                    with ExitStack() as es:
                        wb = [sb(es, "wb%d" % b, [128, KC, 256], BF16) for b in range(2)]
                        ob = [sb(es, "ob%d" % b, [128, 1024], BF16) for b in range(3)]
                        zps = [ps(es, "zps%d" % b, [128, 1024]) for b in range(2)]
                        twb = [kb.dtrk() for _ in range(2)]
                        tob = [kb.dtrk() for _ in range(3)]
                        wfd, R_w = weight("w_in_c", j)
                        wsrc = wfd.ap().rearrange("(k p) c -> p k c", p=128)
                        zc = 0
                        oc = 0
                        for wp in range(16):
                            w = wb[wp % 2]
                            kb.dma("gpsimd", w[:], wsrc[:, :, wp * 256:(wp + 1) * 256], twb[wp % 2], reads=[R_w], writes=[w])
                            for c2 in range(2):
                                cb = 2 * wp + c2
                                for th in range(2):
                                    zp = zps[zc % 2]
                                    zc += 1
                                    for k in range(KC):
                                        for t in range(2):
                                            kb.op("tensor", "matmul", zp[:, t * 512:(t + 1) * 512], w[:, k, c2 * 128:(c2 + 1) * 128],
                                                  hT[:, k, th * 1024 + t * 512: th * 1024 + (t + 1) * 512],
                                                  start=(k == 0), stop=(k == KC - 1), reads=[hT, w], writes=[zp],
                                                  signal=(k == KC - 1 and t == 1))
                                    if cb < 16:
                                        kb.op("scalar", "activation", uT[:, cb, th * 1024:(th + 1) * 1024], zp[:], AF.Copy, reads=[zp], pwrites=[uT])
                                    else:
                                        o = ob[oc % 3]
                                        to = tob[oc % 3]
                                        oc += 1
                                        kb.op("scalar", "activation", o[:], zp[:], AF.Silu, reads=[zp], writes=[o])
                                        gb = cb - 16
                                        kb.dma("sync", gbuf[gb * 128:(gb + 1) * 128, th * 1024:(th + 1) * 1024], o[:], to, reads=[o], pwrites=[R_gbuf])
                        kb.flush()
                with ExitStack() as es:
                    cs = sb(es, "cs", [128, 4, 1024], BF16)
                    oa = [sb(es, "oa%d" % b, [128, T], BF16) for b in range(3)]
                    aps = [ps(es, "aps%d" % b, [128, T]) for b in range(2)]
                    tcs = kb.dtrk()
                    toa = [kb.dtrk() for _ in range(3)]
                    kb.dma("gpsimd", cs[:], cs512_d.ap().rearrange("(k p) c -> p k c", p=128), tcs, writes=[cs])
                    cnt = 0
                    for g in range(4):
                        for mb in range(8):
                            pz = aps[cnt % 2]
                            o = oa[cnt % 3]
                            to = toa[cnt % 3]
                            cnt += 1
                            for cc in range(4):
                                for t in range(4):
                                    kb.op("tensor", "matmul", pz[:, t * 512:(t + 1) * 512], cs[:, cc, mb * 128:(mb + 1) * 128],
                                          uT[:, 4 * g + cc, t * 512:(t + 1) * 512], start=(cc == 0), stop=(cc == 3), reads=[cs, uT], writes=[pz],
                                          signal=(cc == 3 and t == 3))
                            kb.op("scalar", "activation", o[:], pz[:], AF.Copy, reads=[pz], writes=[o])
                            ri = mb // 4
                            mglob = 512 * g + (mb % 4) * 128
                            dj = mglob // 256
                            ml = mglob % 256
                            kb.dma("sync", s3v[dj, ri][ml:ml + 128, :], o[:], to, reads=[o], pwrites=[R_send3])
                    kb.flush()

            if stop == "C1":
                return True
            kb.allgather(send3.ap().opt(), recv3.ap().opt(), reads=[R_send3], writes=[R_recv3])
            r3v = recv3.ap().rearrange("(s j q) c -> j s (q c)", s=8, j=8)

            with ExitStack() as es:
                U = [[sb(es, "U%d%d" % (r, b), [128, 64, 128], BF16) for b in range(2)] for r in range(2)]
                F1 = sb(es, "F1", [128, 256], BF16)
                F2 = sb(es, "F2", [128, 256], BF16)
                TW = sb(es, "TW", [128, 1024], F32)
                Ysb = [sb(es, "Ysb%d" % b, [128, 1024], F32) for b in range(2)]
                tt_ = [[sb(es, "tw%d%d" % (a, b), [128, 512], F32) for b in range(2)] for a in range(4)]
                Ypr = [sb(es, "Ypr%d" % b, [128, 512], BF16) for b in range(2)]
                Ypi = [sb(es, "Ypi%d" % b, [128, 512], BF16) for b in range(2)]
                fo = [sb(es, "fo%d" % b, [128, 64, 128], BF16) for b in range(2)]
                yps = [ps(es, "yps%d" % b, [128, 1024]) for b in range(2)]
                xps = [ps(es, "xps%d" % b, [128, 512]) for b in range(2)]
                tu = [[[kb.dtrk() for s_ in range(2)] for b in range(2)] for r in range(2)]
                tf = [kb.dtrk() for _ in range(3)]
                tfo = [kb.dtrk() for _ in range(2)]
                kb.dma("gpsimd", F1[:], f1_d[:, :], tf[0], writes=[F1])
                kb.dma("gpsimd", F2[:], f2_d[:, :], tf[1], writes=[F2])
                kb.dma("sync", TW[:], tw_d[:, :], tf[2], writes=[TW])
                Tc4 = TW[:, 0:512].rearrange("p (m k) -> p m k", m=4)
                Ts4 = TW[:, 512:1024].rearrange("p (m k) -> p m k", m=4)
                norm = 1.0 / math.sqrt(S * 512.0)
                cnt = 0
                kb.dma("gpsimd", loc3.ap().rearrange("(s q) c -> s (q c)", s=8), lambda e, c: r3v[_rank(e, c)], kb.dtrk(),
                       reads=[R_recv3], writes=[R_loc3])
                l3v = loc3.ap().rearrange("(s r m) c -> s r m c", r=2, s=8)
                for qt in range(4):
                    b = qt % 2
                    for r in range(2):
                        for s_ in range(8):
                            kb.dma("sync", U[r][b][16 * s_:16 * (s_ + 1), :, :],
                                   l3v[s_][r][64 * qt:64 * (qt + 1)].rearrange("m (l n) -> l m n", l=16),
                                   tu[r][b][s_ % 2], reads=[R_loc3], pwrites=[U[r][b]])
                    for ch in range(16):
                        yp = yps[cnt % 2]
                        ys = Ysb[cnt % 2]
                        for mi in range(4):
                            m_ = 4 * ch + mi
                            kb.op("tensor", "matmul", yp[:, mi * 256:(mi + 1) * 256], U[0][b][:, m_, :], F1[:], start=True, stop=False,
                                  reads=[U[0][b], F1], writes=[yp], signal=False)
                            kb.op("tensor", "matmul", yp[:, mi * 256:(mi + 1) * 256], U[1][b][:, m_, :], F2[:], start=False, stop=True,
                                  reads=[U[1][b], F2], writes=[yp], signal=(mi == 3))
                        kb.op("scalar", "activation", ys[:], yp[:], AF.Copy, reads=[yp], writes=[ys])
                        yv = ys[:].rearrange("p (m r k) -> p m r k", m=4, r=2)
                        Yr = yv[:, :, 0, :]
                        Yi = yv[:, :, 1, :]
                        t1, t2, t3, t4 = [tt_[a][cnt % 2] for a in range(4)]
                        v3 = lambda t: t[:].rearrange("p (m k) -> p m k", m=4)
                        kb.op("vector", "tensor_tensor", v3(t1), Yr, Tc4, ALU.mult, reads=[ys, TW], writes=[t1])
                        kb.op("gpsimd", "tensor_tensor", v3(t2), Yi, Ts4, ALU.mult, reads=[ys, TW], writes=[t2])
                        kb.op("vector", "tensor_tensor", v3(t3), Yi, Tc4, ALU.mult, reads=[ys, TW], writes=[t3])
                        kb.op("gpsimd", "tensor_tensor", v3(t4), Yr, Ts4, ALU.mult, reads=[ys, TW], writes=[t4])
                        pr = Ypr[cnt % 2]
                        pi = Ypi[cnt % 2]
                        kb.op("vector", "tensor_tensor", pr[:], t1[:], t2[:], ALU.add, reads=[t1, t2], writes=[pr])
                        kb.op("vector", "tensor_tensor", pi[:], t3[:], t4[:], ALU.subtract, reads=[t3, t4], writes=[pi])
                        xp = xps[cnt % 2]
                        kb.op("tensor", "matmul", xp[:], F2[:, 128:256], pr[:], start=True, stop=False, reads=[F2, pr], writes=[xp], signal=False)
                        kb.op("tensor", "matmul", xp[:], F2[:, 0:128], pi[:], start=False, stop=True, reads=[F2, pi], writes=[xp])
                        kb.op("scalar", "activation", fo[b][:, 4 * ch:4 * ch + 4, :].rearrange("p m k -> p (m k)"), xp[:], AF.Copy, scale=norm,
                              reads=[xp], pwrites=[fo[b]])
                        cnt += 1
                    kb.dma("sync", send4[:, qt * 8192:(qt + 1) * 8192], fo[b][:].rearrange("p m k -> p (m k)"), tfo[b], reads=[fo[b]], pwrites=[R_send4])
                kb.flush()

            if stop == "C2":
                return True
            kb.allgather(send4.ap().opt(), recv4.ap().opt(), reads=[R_send4], writes=[R_recv4])
            r4v = recv4.ap().rearrange("r (a b) -> (r a) b", a=2).rearrange("(j q la) b -> q j la b", j=8, q=8)

            with ExitStack() as es:
                yt = sb(es, "yt", [128, 16, T], BF16)
                ftg = [sb(es, "ftg%d" % b, [128, 4, T], BF16) for b in range(2)]
                wf = [sb(es, "wf%d" % b, [128, 4, 512], BF16) for b in range(2)]
                sg = [sb(es, "sg%d" % b, [128, T], BF16) for b in range(2)]
                wo = [sb(es, "wo%d" % b, [128, KC, 256], BF16) for b in range(2)]
                xin = [sb(es, "xin%d" % b, [128, T], F32) for b in range(2)]
                xo = [sb(es, "xo%d" % b, [128, T], F32) for b in range(2)]
                pp = [ps(es, "pp%d" % b, [128, T]) for b in range(2)]
                tft = [[kb.dtrk() for _ in range(4)] for b in range(2)]
                twf = [kb.dtrk() for _ in range(2)]
                tsg = [kb.dtrk() for _ in range(2)]
                two = [kb.dtrk() for _ in range(2)]
                txi = [kb.dtrk() for _ in range(2)]
                txo = [kb.dtrk() for _ in range(2)]
                kb.dma("gpsimd", loc4.ap().rearrange("r (a b) -> (r a) b", a=2).rearrange("(j la) b -> j la b", j=8), lambda e, c: r4v[_rank(e, c)], kb.dtrk(),
                       reads=[R_recv4], writes=[R_loc4])
                l4v = loc4.ap().rearrange("(j l) c -> j l c", j=8)
                wffd, R_wf = weight("w_fourier", j)
                wfsrc = wffd.ap().rearrange("(g k p) d -> g p k d", g=4, p=128)
                cnt = 0
                for g in range(4):
                    ft = ftg[g % 2]
                    for mc in range(4):
                        blk = 4 * g + mc
                        kb.dma("sync", ft[:, mc, :].rearrange("p (l k) -> p l k", l=16),
                               l4v[blk // 2].rearrange("l (h m k) -> h m l k", h=2, m=128)[blk % 2], tft[g % 2][mc], reads=[R_loc4], pwrites=[ft])
                    kb.dma("gpsimd", wf[g % 2][:], wfsrc[g], twf[g % 2], reads=[R_wf], writes=[wf[g % 2]])
                    for db in range(4):
                        dblk = 4 * g + db
                        pz = pp[cnt % 2]
                        s_ = sg[cnt % 2]
                        kb.dma("sync", s_[:], gbuf[dblk * 128:(dblk + 1) * 128, :], tsg[cnt % 2], reads=[R_gbuf], writes=[s_])
                        cnt += 1
                        for mc in range(4):
                            for t in range(4):
                                kb.op("tensor", "matmul", pz[:, t * 512:(t + 1) * 512], wf[g % 2][:, mc, db * 128:(db + 1) * 128],
                                      ft[:, mc, t * 512:(t + 1) * 512], start=(mc == 0), stop=(mc == 3), reads=[wf[g % 2], ft], writes=[pz],
                                      signal=(mc == 3 and t == 3))
                        kb.op("vector", "tensor_tensor", yt[:, dblk, :], pz[:], s_[:], ALU.mult, reads=[pz, s_], pwrites=[yt])
                wfd, R_w = weight("w_out_c", j)
                wsrc = wfd.ap().rearrange("(k p) c -> p k c", p=128)
                for wp in range(8):
                    w = wo[wp % 2]
                    kb.dma("gpsimd", w[:], wsrc[:, :, wp * 256:(wp + 1) * 256], two[wp % 2], reads=[R_w], writes=[w])
                    for c2 in range(2):
                        db = 2 * wp + c2
                        pz = pp[db % 2]
                        xi = xin[db % 2]
                        x_o = xo[db % 2]
                        kb.dma("sync", xi[:], xsrc[db * 128:(db + 1) * 128, :], txi[db % 2], reads=[R_src], writes=[xi])
                        for k in range(KC):
                            for t in range(4):
                                kb.op("tensor", "matmul", pz[:, t * 512:(t + 1) * 512], w[:, k, c2 * 128:(c2 + 1) * 128],
                                      yt[:, k, t * 512:(t + 1) * 512], start=(k == 0), stop=(k == KC - 1), reads=[w, yt], writes=[pz],
                                      signal=(k == KC - 1 and t == 3))
                        kb.op("vector", "scalar_tensor_tensor", x_o[:], pz[:], mcol(i, 32 + db), xi[:], ALU.mult, ALU.add,
                              reads=[pz, modT, xi], writes=[x_o])
                        kb.dma("sync", xdst[db * 128:(db + 1) * 128, :], x_o[:], txo[db % 2], reads=[x_o], pwrites=[R_dst])
                kb.flush()

        srcs = [(xT, R_xT)]
        bufs = [(xa, R_xa), (xb, R_xb)]
        cur = (xT, R_xT)
        for i in range(NL):
            dst = (yT, R_y) if i == NL - 1 else bufs[i % 2]
            if i % 2 == 0:
                if layer_ab(i, cur[0], cur[1], dst[0], dst[1]):
                    break
            else:
                if layer_c(i, cur[0], cur[1], dst[0], dst[1]):
                    break
            cur = dst
        if debug:
            dl = [("send1", send1, R_send1), ("send2", send2, R_send2), ("gbuf", gbuf, R_gbuf)]
            if NL > 1:
                dl += [("send3", send3, R_send3), ("send4", send4, R_send4)]
            for nm, src, Rs in dl:
                if nm in debug:
                    shp = list(src.shape)
                    dbg = nc.dram_tensor("dbg_" + nm, shp, BF16, kind="ExternalOutput")
                    kb.dma("sync", dbg.ap().rearrange("(a p) c -> p a c", p=128), src.ap().rearrange("(a p) c -> p a c", p=128), kb.dtrk(), reads=[Rs])
            kb.flush()
    return nc, used_inputs


def _host_inputs(inp, r):
    f = np.float32
    x = np.asarray(inp["x"], f)[0]
    m = {}
    m["xT"] = np.ascontiguousarray(x[r * T:(r + 1) * T, :].T)
    c = np.asarray(inp["c"], f)[0]
    m["cvec"] = np.ascontiguousarray(c.reshape(16, 128).T)
    nw = np.asarray(inp["norm_w"], f)
    m["nw"] = np.ascontiguousarray(nw.reshape(4, 16, 128).transpose(2, 0, 1).reshape(128, 64))
    m["adaw"] = np.ascontiguousarray(np.asarray(inp["ada_w"], f)[:, :, r * 768:(r + 1) * 768])
    ab = np.asarray(inp["ada_b"], f)[:, r * 768:(r + 1) * 768]
    m["adab"] = np.ascontiguousarray(ab.reshape(4, 6, 128).transpose(2, 0, 1).reshape(128, 24))
    for k in ["w_in_ab", "w_pool", "w_out_ab", "w_in_c", "w_fourier", "w_out_c"]:
        wa = np.asarray(inp[k], f)
        for jj in range(2):
            w2 = wa[jj].reshape(-1, wa.shape[-1])
            n = w2.shape[0] // NCORES
            m["%s_%d" % (k, jj)] = np.ascontiguousarray(w2[r * n:(r + 1) * n])
    psc = np.asarray(inp["pool_scale"], f)
    m["pscale"] = np.ascontiguousarray(psc.reshape(2, 8, 128).transpose(2, 0, 1).reshape(128, 16))
    qn = np.asarray(inp["q_norm_w"], f)
    kn = np.asarray(inp["k_norm_w"], f)
    qkw = np.zeros((128, 4), f)
    for j in range(2):
        qkw[:, 2 * j] = np.tile(qn[j], 2)
        qkw[:, 2 * j + 1] = np.tile(kn[j], 2)
    m["qkw"] = qkw
    lam = np.stack([np.asarray(inp[k], f) for k in ["lambda_q1", "lambda_k1", "lambda_q2", "lambda_k2"]], axis=1)
    m["lamv"] = np.ascontiguousarray(np.broadcast_to(lam.reshape(1, 512), (128, 512)))
    m["subw"] = np.ascontiguousarray(np.asarray(inp["subln_w"], f).T)
    slope = 2.0 ** (-(r + 1))
    t = np.arange(S)
    m["kaug"] = np.stack([np.ones(S), slope * (t % 128)]).astype(f)
    q2 = (np.arange(512) - 256).astype(np.float64)
    m["qaug"] = np.stack([-slope * q2, np.ones(512), slope * q2, -np.ones(512)]).astype(f)
    ab_ = np.zeros(257)
    for n in range(1, 129):
        ab_[n] = -slope * (128 * n + 256)
    for n in range(4, 128):
        ab_[129 + n] = -slope * (128 * n - 256)
    m["abias"] = np.ascontiguousarray(np.broadcast_to(ab_.astype(f), (128, 257)))
    kk = np.arange(128)[:, None, None]
    rr = np.arange(4)[None, :, None]
    qq = np.arange(512)[None, None, :]
    m["dtab"] = (-slope * np.abs(qq - (128 * rr + kk))).astype(f).reshape(128, 2048)
    wins = (2, 4, 8, 16)
    w = wins[r // 2]
    pc = np.array([1.0 / ww if ww == w else 0.0 for ww in wins], f)
    m["pcoef"] = np.ascontiguousarray(np.broadcast_to(pc, (128, 4)))
    te = np.concatenate([np.arange(8), np.arange(S - 8, S)])
    lo = np.clip(te - w // 2, 0, S - 1)
    hi = np.clip(te + (w - w // 2) - 1, 0, S - 1)
    ratio = (w / (hi - lo + 1)).astype(f)
    m["pratio"] = np.ascontiguousarray(np.broadcast_to(ratio, (128, 16)))
    cc = np.arange(512)[:, None] * np.arange(512)[None, :]
    ang = 2.0 * np.pi * (cc % 512) / 512.0
    m["cs512"] = np.concatenate([np.cos(ang), -np.sin(ang)], axis=1).astype(f)
    aa = np.arange(128)[:, None] * np.arange(128)[None, :]
    a128 = 2.0 * np.pi * (aa % 128) / 128.0
    Fc, Fs = np.cos(a128), np.sin(a128)
    m["f1c"] = np.concatenate([Fc, -Fs], axis=1).astype(f)
    m["f2c"] = np.concatenate([Fs, Fc], axis=1).astype(f)
    at = 2.0 * np.pi * aa / float(S)
    m["twc"] = np.concatenate([np.tile(np.cos(at), (1, 4)), np.tile(np.sin(at), (1, 4))], axis=1).astype(f)
    return m


_NC_CACHE = {}


def kernel(**inputs):
    NL = 4
    if NL not in _NC_CACHE:
        _NC_CACHE[NL] = build(NL)
    nc, used = _NC_CACHE[NL]
    in_maps = []
    for r in range(NCORES):
        hm = _host_inputs(inputs, r)
        in_maps.append({k: hm[k] for k in used})
    res = run_bass_kernel_spmd(nc, in_maps, core_ids=list(range(NCORES)))
    out = np.concatenate([np.asarray(res.results[r]["yT"]).T for r in range(NCORES)], axis=0)
    return out[None].astype(np.float32)
```

```python
import math
from contextlib import ExitStack

import numpy as np
import concourse.bass as bass
import concourse.mybir as mybir
from concourse.bass_utils import run_bass_kernel_spmd

F32 = mybir.dt.float32
BF16 = mybir.dt.bfloat16
ALU = mybir.AluOpType
AF = mybir.ActivationFunctionType
AX = mybir.AxisListType

NCORES = 8
S = 16384
D = 2048
T = 2048
KC = 16
EPS = 1e-6
ENG = ["sync", "scalar", "vector", "gpsimd", "tensor"]
NDMA = 56


class Trk:
    def __init__(self, nc, name):
        self.sem = nc.alloc_semaphore(name=name)
        self.count = 0
        self.name = name


class Res:
    def __init__(self, name):
        self.name = name
        self.w = {}
        self.r = {}


class Tile(Res):
    def __init__(self, name, handle):
        super().__init__(name)
        self.h = handle

    def __getitem__(self, idx):
        return self.h[idx]


class KB:
    def __init__(self, nc):
        self.nc = nc
        self.etrk = {e: Trk(nc, "e_" + e) for e in ["scalar", "vector", "gpsimd", "tensor"]}
        self.seen = {e: {} for e in ENG}
        self.ops = {e: [] for e in ENG}
        self.dpool = [Trk(nc, "d%d" % i) for i in range(NDMA)]
        self.dfree = list(self.dpool)
        self.cc = []
        self.nblk = 0

    def dtrk(self):
        return self.dfree.pop()

    def _need(self, eng, waits, trk, val):
        if val <= 0:
            return
        if eng == "tensor" and trk is self.etrk["tensor"]:
            return
        if self.seen[eng].get(trk, 0) >= val:
            return
        self.seen[eng][trk] = val
        waits.append((trk.sem, val))

    def _deps(self, eng, reads, writes, pwrites=()):
        waits = []
        for r in reads:
            for t, v in r.w.items():
                self._need(eng, waits, t, v)
        for w in writes:
            for t, v in w.w.items():
                self._need(eng, waits, t, v)
            for t, v in w.r.items():
                self._need(eng, waits, t, v)
        for w in pwrites:
            for t, v in w.r.items():
                self._need(eng, waits, t, v)
        return waits

    def _reg(self, v, reads, writes, pwrites=()):
        for r in reads:
            if r.r.get(v[0], 0) < v[1]:
                r.r[v[0]] = v[1]
        for w in writes:
            w.w = {v[0]: v[1]}
            w.r = {}
        for w in pwrites:
            if w.w.get(v[0], 0) < v[1]:
                w.w[v[0]] = v[1]

    def op(self, eng, meth, *args, reads=(), writes=(), pwrites=(), signal=True, **kw):
        waits = self._deps(eng, reads, writes, pwrites)
        trk = self.etrk[eng]
        if signal:
            trk.count += 1
            val = trk.count
            inc = (trk.sem, 1)
        else:
            val = trk.count + 1
            inc = None
        self._reg((trk, val), reads, writes, pwrites)
        self.ops[eng].append((meth, args, kw, waits, inc))

    def dma(self, eng, out, in_, trk, reads=(), writes=(), pwrites=(), **kw):
        waits = self._deps(eng, reads, writes, pwrites)
        self._need(eng, waits, trk, trk.count)
        trk.count += 16
        self._reg((trk, trk.count), reads, writes, pwrites)
        kw = dict(kw)
        kw["out"] = out
        kw["in_"] = in_
        self.ops[eng].append(("dma_start", (), kw, waits, (trk.sem, 16)))

    def allgather(self, src, dst, reads, writes):
        eng = "gpsimd"
        waits = self._deps(eng, reads, writes)
        trk = Trk(self.nc, "cc%d" % len(self.cc))
        self.cc.append(trk)
        trk.count = 1
        self._reg((trk, 1), reads, writes)
        kw = dict(replica_groups=[list(range(NCORES))], ins=[src], outs=[dst])
        self.ops[eng].append(("collective_compute", ("AllGather", ALU.bypass), kw, waits, (trk.sem, None)))

    def flush(self):
        waits = []
        for t in list(self.etrk.values()) + self.dpool + self.cc:
            self._need("sync", waits, t, t.count)
        self.ops["sync"].append((None, (), {}, waits, None))
        self.nblk += 1
        with self.nc.Block("blk%d" % self.nblk) as block:
            for e in ENG:
                ops = self.ops[e]
                if not ops:
                    continue

                def body(eng, ops=ops):
                    ctx = {}
                    for meth, args, kw, waits, inc in ops:
                        for sem, val in waits:
                            eng.wait_ge(sem, val)
                        if meth is None:
                            continue
                        a = [x(eng, ctx) if callable(x) else x for x in args]
                        k = {n: (x(eng, ctx) if callable(x) else x) for n, x in kw.items()}
                        try:
                            ins = getattr(eng, meth)(*a, **k)
                        except Exception:
                            print("FAILED OP", meth, [getattr(x, "shape", x) for x in a], {n: getattr(x, "shape", x) for n, x in k.items()})
                            raise
                        if inc is not None:
                            if inc[1] is None:
                                ins.then_inc(inc[0])
                            else:
                                ins.then_inc(inc[0], inc[1])

                getattr(block, e)(body)
        self.ops = {e: [] for e in ENG}
        self.dfree = list(self.dpool)


_RANK = {}


def _rank(eng, ctx):
    if "rank" not in ctx:
        ctx["rank"] = eng.partition_id()
    return ctx["rank"]


def build(NL=4, debug=False, stop=None):
    nc = bass.Bass("TRN2", target_bir_lowering=False)
    _RANK.clear()
    kb = KB(nc)

    used_inputs = []

    def din(name, shape, dt=F32):
        used_inputs.append(name)
        return nc.dram_tensor(name, list(shape), dt, kind="ExternalInput")

    xT = din("xT", [D, T])
    cvec = din("cvec", [128, 16])
    nw_d = din("nw", [128, 64])
    adaw = din("adaw", [4, D, 768])
    adab = din("adab", [128, 24])
    pscale_d = din("pscale", [128, 16])
    qkw_d = din("qkw", [128, 4])
    lamv_d = din("lamv", [128, 2 * 4 * 64])
    subw_d = din("subw", [128, 2])
    kaug_d = din("kaug", [2, S])
    qaug_d = din("qaug", [4, 512])
    abias_d = din("abias", [128, 257])
    dtab_d = din("dtab", [128, 4 * 512])
    pcoef_d = din("pcoef", [128, 4])
    pratio_d = din("pratio", [128, 16])
    WSHAPE = {"w_in_ab": (D, 6144), "w_pool": (1024, 256), "w_out_ab": (D, D), "w_in_c": (D, 4096),
              "w_fourier": (2048, 512), "w_out_c": (D, D)}
    wfull = {}

    wdecl = {}

    def declare_weight(name, j):
        key = (name, j)
        if key not in wdecl:
            rows, cols = WSHAPE[name]
            nm = "%s_%d" % (name, j)
            sh = din(nm, [rows // NCORES, cols])
            bounce = nc.dram_tensor(nm + "_b", [rows // NCORES, cols], F32)
            full = nc.dram_tensor(nm + "_f", [rows, cols], F32)
            wdecl[key] = (sh, bounce, full, Res(nm + "_b"), Res(nm + "_f"))
        return wdecl[key]

    def weight(name, j):
        key = (name, j)
        sh, bounce, full, Rb, Rf = declare_weight(name, j)
        if key not in wfull:
            kb.dma("sync", bounce[:, :], sh[:, :], kb.dtrk(), writes=[Rb])
            kb.allgather(bounce.ap().opt(), full.ap().opt(), reads=[Rb], writes=[Rf])
            wfull[key] = (full, Rf)
        return wfull[key]

    def needed_weights():
        out = []
        for li_ in range(NL):
            if li_ % 2 == 0:
                out.append(("w_in_ab", li_ // 2))
                if stop is None or stop.startswith("C"):
                    out += [("w_pool", li_ // 2), ("w_out_ab", li_ // 2)]
            else:
                out.append(("w_in_c", li_ // 2))
                if stop is None:
                    out += [("w_fourier", li_ // 2), ("w_out_c", li_ // 2)]
        return out

    for nm_, j_ in needed_weights():
        declare_weight(nm_, j_)

    yT = nc.dram_tensor("yT", [D, T], F32, kind="ExternalOutput") if stop is None else nc.dram_tensor("yT", [D, T], F32)

    modsend = nc.dram_tensor("modsend", [128, 24], F32)
    modall = nc.dram_tensor("modall", [NCORES * 128, 24], F32)
    xa = nc.dram_tensor("xa", [D, T], F32)
    xb = nc.dram_tensor("xb", [D, T], F32)
    gbuf = nc.dram_tensor("gbuf", [D, T], BF16)
    send1 = nc.dram_tensor("send1", [4096, 2048], BF16)
    recv1 = nc.dram_tensor("recv1", [NCORES * 4096, 2048], BF16)
    loc1 = nc.dram_tensor("loc1", [512, S], BF16)
    send2 = nc.dram_tensor("send2", [256, S], BF16)
    recv2 = nc.dram_tensor("recv2", [NCORES * 256, S], BF16)
    R_xT, R_xa, R_xb, R_y = Res("xT"), Res("xa"), Res("xb"), Res("yT")
    R_gbuf, R_send1, R_recv1, R_loc1 = Res("gbuf"), Res("send1"), Res("recv1"), Res("loc1")
    R_send2, R_recv2 = Res("send2"), Res("recv2")
    R_modsend, R_modall = Res("modsend"), Res("modall")
    if NL > 1:
        send3 = nc.dram_tensor("send3", [4096, 2048], BF16)
        recv3 = nc.dram_tensor("recv3", [NCORES * 4096, 2048], BF16)
        send4 = nc.dram_tensor("send4", [128, 32768], BF16)
        recv4 = nc.dram_tensor("recv4", [NCORES * 128, 32768], BF16)
        R_send3, R_recv3, R_send4, R_recv4 = Res("send3"), Res("recv3"), Res("send4"), Res("recv4")
        loc3 = nc.dram_tensor("loc3", [4096, 2048], BF16)
        loc4 = nc.dram_tensor("loc4", [128, 32768], BF16)
        R_loc3, R_loc4 = Res("loc3"), Res("loc4")
        cs512_d = din("cs512", [512, 1024])
        f1_d = din("f1c", [128, 256])
        f2_d = din("f2c", [128, 256])
        tw_d = din("twc", [128, 1024])

    with ExitStack() as glob:
        uid = [0]

        def sb(es, name, shape, dt):
            uid[0] += 1
            name = "s%d_%s" % (uid[0], name)
            return Tile(name, es.enter_context(nc.sbuf_tensor(name, list(shape), dt)))

        def ps(es, name, shape):
            uid[0] += 1
            name = "p%d_%s" % (uid[0], name)
            return Tile(name, es.enter_context(nc.psum_tensor(name, list(shape), F32)))

        ones = sb(glob, "ones", [128, 128], BF16)
        bones = sb(glob, "bones", [128, 128], BF16)
        ones32 = sb(glob, "ones32", [128, 128], F32)
        modT = sb(glob, "modT", [128, NCORES * 24], F32)
        nwt = sb(glob, "nwt", [128, 64], F32)
        acoef = sb(glob, "acoef", [128, 16], F32)
        pscale = sb(glob, "pscale_t", [128, 16], F32)
        qkw = sb(glob, "qkw_t", [128, 4], F32)
        subw = sb(glob, "subw_t", [128, 2], F32)
        lamt = sb(glob, "lamt", [128, 8], F32)
        pcoef = sb(glob, "pcoef_t", [128, 4], F32)
        pratio = sb(glob, "pratio_t", [128, 16], F32)

        def mcol(i, ci):
            c = (ci // 6) * 24 + i * 6 + (ci % 6)
            return modT[:, c:c + 1]

        with ExitStack() as es:
            cv = sb(es, "cv", [128, 16], F32)
            cact = sb(es, "cact", [128, 16], F32)
            adab_t = sb(es, "adab_t", [128, 24], F32)
            modsl = sb(es, "modsl", [128, 24], F32)
            aw = [sb(es, "aw%d" % b, [128, 16, 768], F32) for b in range(2)]
            lamv = sb(es, "lamv", [128, 512], F32)
            lprod = sb(es, "lprod", [128, 512], F32)
            lsum = sb(es, "lsum", [128, 8], F32)
            mps = ps(es, "mps", [128, 512])
            trk = [kb.dtrk() for _ in range(12)]
            kb.op("vector", "memset", ones[:], 1.0, writes=[ones])
            kb.op("vector", "memset", ones32[:], 1.0, writes=[ones32])
            kb.op("vector", "memset", bones[:], 0.0, writes=[bones])
            kb.op("vector", "memset", bones[0:64, 0:64], 1.0, writes=[bones])
            kb.op("vector", "memset", bones[64:128, 64:128], 1.0, writes=[bones])
            kb.dma("sync", cv[:], cvec[:, :], trk[0], writes=[cv])
            kb.dma("sync", adab_t[:], adab[:, :], trk[1], writes=[adab_t])
            kb.dma("sync", nwt[:], nw_d[:, :], trk[2], writes=[nwt])
            kb.dma("sync", pscale[:], pscale_d[:, :], trk[3], writes=[pscale])
            kb.dma("sync", qkw[:], qkw_d[:, :], trk[4], pwrites=[qkw])
            kb.dma("sync", subw[:], subw_d[:, :], trk[5], writes=[subw])
            kb.dma("sync", lamv[:], lamv_d[:, :], trk[6], writes=[lamv])
            kb.dma("sync", pcoef[:], pcoef_d[:, :], trk[7], writes=[pcoef])
            kb.dma("sync", pratio[:], pratio_d[:, :], trk[8], writes=[pratio])
            kb.op("scalar", "activation", cact[:], cv[:], AF.Silu, reads=[cv], writes=[cact])
            for i in range(4):
                a = aw[i % 2]
                kb.dma("sync", a[:], adaw[i].rearrange("(k p) c -> p k c", p=128), trk[9 + i % 2], writes=[a])
                for b in range(6):
                    col = i * 6 + b
                    for k in range(KC):
                        kb.op("tensor", "matmul", mps[:, col:col + 1], a[:, k, b * 128:(b + 1) * 128], cact[:, k:k + 1],
                              start=(k == 0), stop=(k == KC - 1), reads=[a, cact], writes=[mps], signal=(k == KC - 1))
            kb.op("vector", "tensor_tensor", modsl[:], mps[:, 0:24], adab_t[:], ALU.add, reads=[mps, adab_t], writes=[modsl])
            kb.dma("sync", modsend[:, :], modsl[:], trk[11], reads=[modsl], writes=[R_modsend])
            kb.allgather(modsend.ap().opt(), modall.ap().opt(), reads=[R_modsend], writes=[R_modall])
            kb.dma("gpsimd", modT[:].rearrange("p (r c) -> p r c", c=24), modall.ap().rearrange("(r p) c -> p r c", p=128),
                   trk[0], reads=[R_modall], writes=[modT])
            kb.op("vector", "tensor_scalar", qkw[:, 0:1], qkw[:, 0:1], 0.125, None, ALU.mult, reads=[qkw], pwrites=[qkw])
            kb.op("vector", "tensor_scalar", qkw[:, 2:3], qkw[:, 2:3], 0.125, None, ALU.mult, reads=[qkw], pwrites=[qkw])
            for j in range(2):
                for t in range(2):
                    base = j * 256 + t * 128
                    kb.op("vector", "tensor_tensor", lprod[:, base:base + 64], lamv[:, base:base + 64],
                          lamv[:, base + 64:base + 128], ALU.mult, reads=[lamv], pwrites=[lprod])
                    kb.op("vector", "tensor_reduce", lsum[:, 2 * j + t:2 * j + t + 1], lprod[:, base:base + 64], AX.X, ALU.add,
                          reads=[lprod], pwrites=[lsum])
            kb.op("scalar", "activation", lsum[:, 0:4], lsum[:, 0:4], AF.Exp, reads=[lsum], pwrites=[lsum])
            for j in range(2):
                li = 0.8 - 0.6 * math.exp(-0.3 * (2 * j))
                kb.op("vector", "tensor_tensor", lamt[:, 2 * j:2 * j + 1], lsum[:, 2 * j + 1:2 * j + 2], lsum[:, 2 * j:2 * j + 1],
                      ALU.subtract, reads=[lsum], pwrites=[lamt])
                kb.op("vector", "tensor_scalar", lamt[:, 2 * j:2 * j + 1], lamt[:, 2 * j:2 * j + 1], -li, None, ALU.add,
                      reads=[lamt], pwrites=[lamt])
                kb.op("vector", "tensor_scalar", lamt[:, 2 * j + 1:2 * j + 2], subw[:, j:j + 1], 1.0 - li, None, ALU.mult,
                      reads=[subw], pwrites=[lamt])
            weight("w_in_ab", 0)
            kb.flush()

        def stage_in(i, xsrc, R_src, hT):
            with ExitStack() as es:
                xs = [sb(es, "xs%d" % b, [128, T], F32) for b in range(2)]
                sq = [sb(es, "sq%d" % b, [128, T], BF16) for b in range(2)]
                rstd = sb(es, "rstd", [128, T], F32)
                tmp = [sb(es, "tmpx%d" % b, [128, T], F32) for b in range(2)]
                ssps = ps(es, "ssps", [128, T])
                tx = [kb.dtrk() for _ in range(2)]
                for k in range(KC):
                    kb.op("vector", "scalar_tensor_tensor", acoef[:, k:k + 1], mcol(i, 16 + k), 1.0, nwt[:, i * 16 + k:i * 16 + k + 1],
                          ALU.add, ALU.mult, reads=[modT, nwt], pwrites=[acoef])
                for k in range(KC):
                    b = k % 2
                    kb.dma("sync", xs[b][:], xsrc[k * 128:(k + 1) * 128, :], tx[b], reads=[R_src], writes=[xs[b]])
                    kb.op("scalar", "activation", sq[b][:], xs[b][:], AF.Square, reads=[xs[b]], writes=[sq[b]])
                    for t in range(4):
                        kb.op("tensor", "matmul", ssps[:, t * 512:(t + 1) * 512], ones[:], sq[b][:, t * 512:(t + 1) * 512],
                              start=(k == 0), stop=(k == KC - 1), reads=[sq[b], ones], writes=[ssps], signal=(t == 3))
                kb.op("scalar", "activation", rstd[:], ssps[:], AF.Sqrt, bias=EPS, scale=1.0 / D, reads=[ssps], writes=[rstd])
                kb.op("vector", "reciprocal", rstd[:], rstd[:], reads=[rstd], writes=[rstd])
                for k in range(KC):
                    b = k % 2
                    kb.dma("sync", xs[b][:], xsrc[k * 128:(k + 1) * 128, :], tx[b], reads=[R_src], writes=[xs[b]])
                    kb.op("vector", "tensor_tensor", tmp[b][:], xs[b][:], rstd[:], ALU.mult, reads=[xs[b], rstd], writes=[tmp[b]])
                    kb.op("vector", "tensor_scalar", hT[:, k, :], tmp[b][:], acoef[:, k:k + 1], mcol(i, k), ALU.mult, ALU.add,
                          reads=[tmp[b], acoef, modT], pwrites=[hT])
                kb.flush()

        def layer_ab(i, xsrc, R_src, xdst, R_dst):
            j = i // 2
            s1v = send1.ap().rearrange("(j t p) c -> j t p c", j=8, t=4)

            with ExitStack() as outer:
                hT = sb(outer, "hT", [128, KC, T], BF16)
                stage_in(i, xsrc, R_src, hT)
                with ExitStack() as es:
                    wb = [sb(es, "wb%d" % b, [128, KC, 256], BF16) for b in range(2)]
                    ob = [sb(es, "ob%d" % b, [128, 1024], BF16) for b in range(3)]
                    sqq = [sb(es, "sqq%d" % b, [128, 1024], BF16) for b in range(2)]
                    rq = [sb(es, "rq%d" % b, [128, 1024], F32) for b in range(2)]
                    vb = [sb(es, "vb%d" % b, [128, 256], BF16) for b in range(2)]
                    zps = [ps(es, "zps%d" % b, [128, 1024]) for b in range(2)]
                    ssq = [ps(es, "ssq%d" % b, [128, 1024]) for b in range(2)]
                    twb = [kb.dtrk() for _ in range(2)]
                    tob = [kb.dtrk() for _ in range(3)]
                    tvb = [kb.dtrk() for _ in range(4)]
                    wfd, R_w = weight("w_in_ab", j)
                    wsrc = wfd.ap().rearrange("(k p) c -> p k c", p=128)
                    zc = 0
                    oc = 0
                    vc = 0
                    for wp in range(24):
                        if wp == 16 and stop != "A1":
                            kb.allgather(send1.ap().opt(), recv1.ap().opt(), reads=[R_send1], writes=[R_recv1])
                        w = wb[wp % 2]
                        kb.dma("gpsimd", w[:], wsrc[:, :, wp * 256:(wp + 1) * 256], twb[wp % 2], reads=[R_w], writes=[w])
                        if 12 <= wp < 16:
                            for tt in range(16):
                                zp = zps[zc % 2]
                                zc += 1
                                for k in range(KC):
                                    kb.op("tensor", "matmul", zp[:, 0:256], hT[:, k, tt * 128:(tt + 1) * 128], w[:, k, :],
                                          start=(k == 0), stop=(k == KC - 1), reads=[hT, w], writes=[zp], signal=(k == KC - 1))
                                v = vb[vc % 2]
                                kb.op("scalar", "activation", v[:], zp[:, 0:256], AF.Copy, reads=[zp], writes=[v])
                                for hl in range(2):
                                    hd = 2 * (wp - 12) + hl
                                    dst = s1v[hd, 3].rearrange("p (t e) -> p t e", e=128)[:, tt, :]
                                    kb.dma("sync", dst, v[:, hl * 128:(hl + 1) * 128], tvb[(2 * vc + hl) % 4], reads=[v], pwrites=[R_send1])
                                vc += 1
                            continue
                        for c2 in range(2):
                            cb = 2 * wp + c2
                            for th in range(2):
                                zp = zps[zc % 2]
                                zc += 1
                                for k in range(KC):
                                    for t in range(2):
                                        kb.op("tensor", "matmul", zp[:, t * 512:(t + 1) * 512], w[:, k, c2 * 128:(c2 + 1) * 128],
                                              hT[:, k, th * 1024 + t * 512: th * 1024 + (t + 1) * 512],
                                              start=(k == 0), stop=(k == KC - 1), reads=[hT, w], writes=[zp],
                                              signal=(k == KC - 1 and t == 1))
                                o = ob[oc % 3]
                                to = tob[oc % 3]
                                oc += 1
                                if cb < 8 or cb >= 32:
                                    if cb < 8:
                                        kb.op("scalar", "activation", o[:], zp[:], AF.Copy, reads=[zp], writes=[o])
                                        dst = s1v[cb, 2][:, th * 1024:(th + 1) * 1024]
                                        kb.dma("sync", dst, o[:], to, reads=[o], pwrites=[R_send1])
                                    else:
                                        kb.op("scalar", "activation", o[:], zp[:], AF.Silu, reads=[zp], writes=[o])
                                        gb = cb - 32
                                        dst = gbuf[gb * 128:(gb + 1) * 128, th * 1024:(th + 1) * 1024]
                                        kb.dma("sync", dst, o[:], to, reads=[o], pwrites=[R_gbuf])
                                else:
                                    isq = cb < 16
                                    hd = cb - 8 if isq else cb - 16
                                    sq_ = sqq[oc % 2]
                                    r_ = rq[oc % 2]
                                    sp = ssq[oc % 2]
                                    kb.op("scalar", "activation", sq_[:], zp[:], AF.Square, reads=[zp], writes=[sq_])
                                    for t in range(2):
                                        kb.op("tensor", "matmul", sp[:, t * 512:(t + 1) * 512], bones[:], sq_[:, t * 512:(t + 1) * 512],
                                              start=True, stop=True, reads=[bones, sq_], writes=[sp], signal=(t == 1))
                                    kb.op("scalar", "activation", r_[:], sp[:], AF.Sqrt, bias=EPS, scale=1.0 / 64, reads=[sp], writes=[r_])
                                    kb.op("vector", "reciprocal", r_[:], r_[:], reads=[r_], writes=[r_])
                                    wcol = 2 * j + (0 if isq else 1)
                                    kb.op("vector", "scalar_tensor_tensor", o[:], zp[:], qkw[:, wcol:wcol + 1], r_[:], ALU.mult, ALU.mult,
                                          reads=[zp, qkw, r_], writes=[o])
                                    dst = s1v[hd, 0 if isq else 1][:, th * 1024:(th + 1) * 1024]
                                    kb.dma("sync", dst, o[:], to, reads=[o], pwrites=[R_send1])
                    kb.flush()

            if stop == "A1":
                return True
            r1all = recv1.ap().rearrange("(s j r) c -> j r s c", s=8, j=8)
            s2v = send2.ap().rearrange("(t p) c -> t p c", t=2)

            with ExitStack() as es:
                up = sb(es, "up", [128, S + 32], BF16)
                P_ = sb(es, "poolP", [128, 4096 + 16], F32)
                Q_ = sb(es, "poolQ", [128, 4096 + 16], F32)
                A_ = sb(es, "poolA", [128, 4096 + 16], F32)
                po = [sb(es, "po%d" % b, [128, 4096], BF16) for b in range(2)]
                tu = kb.dtrk()
                tq = kb.dtrk()
                tpo = [kb.dtrk() for _ in range(2)]
                kb.op("vector", "memset", up[:, 0:16], 0.0, pwrites=[up])
                kb.op("vector", "memset", up[:, 16 + S:32 + S], 0.0, pwrites=[up])
                kb.dma("gpsimd", loc1.ap().rearrange("r (s c) -> r s c", s=8),
                       lambda e, c: r1all[_rank(e, c)], tq, reads=[R_recv1], writes=[R_loc1])
                kb.dma("sync", up[:, 16:16 + S], loc1[256:384, :], tu, reads=[R_loc1], pwrites=[up])
                W = 4096 + 16
                for c in range(4):
                    base = 16 + 4096 * c - 8
                    kb.op("vector", "tensor_tensor", P_[:, 0:W], up[:, base:base + W], up[:, base - 1:base - 1 + W], ALU.add,
                          reads=[up], writes=[P_])
                    kb.op("vector", "tensor_scalar", A_[:, 0:W], P_[:, 0:W], pcoef[:, 0:1], None, ALU.mult, reads=[P_, pcoef], writes=[A_])
                    kb.op("vector", "tensor_tensor", Q_[:, 1:W - 1], P_[:, 2:W], P_[:, 0:W - 2], ALU.add, reads=[P_], writes=[Q_])
                    kb.op("vector", "scalar_tensor_tensor", A_[:, 1:W - 1], Q_[:, 1:W - 1], pcoef[:, 1:2], A_[:, 1:W - 1], ALU.mult, ALU.add,
                          reads=[Q_, pcoef, A_], writes=[A_])
                    kb.op("vector", "tensor_tensor", P_[:, 3:W - 3], Q_[:, 5:W - 1], Q_[:, 1:W - 5], ALU.add, reads=[Q_], writes=[P_])
                    kb.op("vector", "scalar_tensor_tensor", A_[:, 3:W - 3], P_[:, 3:W - 3], pcoef[:, 2:3], A_[:, 3:W - 3], ALU.mult, ALU.add,
                          reads=[P_, pcoef, A_], writes=[A_])
                    kb.op("vector", "tensor_tensor", Q_[:, 7:W - 7], P_[:, 11:W - 3], P_[:, 3:W - 11], ALU.add, reads=[P_], writes=[Q_])
                    kb.op("vector", "scalar_tensor_tensor", A_[:, 7:W - 7], Q_[:, 7:W - 7], pcoef[:, 3:4], A_[:, 7:W - 7], ALU.mult, ALU.add,
                          reads=[Q_, pcoef, A_], writes=[A_])
                    if c == 0:
                        kb.op("vector", "tensor_tensor", A_[:, 8:16], A_[:, 8:16], pratio[:, 0:8], ALU.mult, reads=[A_, pratio], writes=[A_])
                    if c == 3:
                        kb.op("vector", "tensor_tensor", A_[:, 8 + 4088:8 + 4096], A_[:, 8 + 4088:8 + 4096], pratio[:, 8:16], ALU.mult,
                              reads=[A_, pratio], writes=[A_])
                    o = po[c % 2]
                    kb.op("vector", "tensor_tensor", o[:], A_[:, 8:8 + 4096], up[:, 16 + 4096 * c:16 + 4096 * (c + 1)], ALU.subtract,
                          reads=[A_, up], writes=[o])
                    kb.dma("sync", s2v[0][:, 4096 * c:4096 * (c + 1)], o[:], tpo[c % 2], reads=[o], pwrites=[R_send2])
                kb.flush()

            if stop == "A2a":
                return True
            with ExitStack() as es:
                KT = [sb(es, "KT%d" % m, [66, S], BF16) for m in range(2)]
                V = sb(es, "V", [128, 128, 128], BF16)
                QS = [[[sb(es, "Q%s%d%d" % (v, m, b), [66, 512], BF16) for b in range(2)] for m in range(2)] for v in "LR"]
                abias = sb(es, "abias", [128, 257], F32)
                dtab = sb(es, "dtab", [128, 4, 512], F32)
                Pb = [sb(es, "Pb%d" % b, [128, 1024], BF16) for b in range(3)]
                scs = [sb(es, "scs%d" % b, [128, 1024], F32) for b in range(2)]
                accA = [sb(es, "accA%d" % b, [128, 1024], F32) for b in range(2)]
                acch = sb(es, "acch", [128, 1024], BF16)
                accl = sb(es, "accl", [128, 1024], BF16)
                r1 = sb(es, "r1", [128, 512], F32)
                r2 = sb(es, "r2", [128, 512], F32)
                o1 = sb(es, "o1", [128, 512], F32)
                o2 = sb(es, "o2", [128, 512], F32)
                od = sb(es, "od", [128, 512], F32)
                osq = sb(es, "osq", [128, 512], BF16)
                rr = sb(es, "rr", [128, 512], F32)
                on = [sb(es, "on%d" % b, [128, 512], BF16) for b in range(2)]
                scp = [ps(es, "scp%d" % b, [128, 1024]) for b in range(2)]
                Op = [ps(es, "Op%d" % m, [128, 512]) for m in range(2)]
                Lp = [ps(es, "Lp%d" % m, [128, 512]) for m in range(2)]
                tk = [kb.dtrk() for _ in range(8)]
                tqs = [[[kb.dtrk() for b in range(2)] for m in range(2)] for v in range(2)]
                ton = [kb.dtrk() for _ in range(2)]
                for m in range(2):
                    kb.dma("sync", KT[m][0:64, :], loc1[128 + 64 * m:128 + 64 * (m + 1), :], tk[m], reads=[R_loc1], pwrites=[KT[m]])
                    kb.dma("gpsimd", KT[m][64:66, :].rearrange("p (a c) -> p a c", c=2048), kaug_d.ap().rearrange("p (a c) -> p a c", c=2048), tk[2 + m], pwrites=[KT[m]])
                kb.dma("sync", V[:].rearrange("p a e -> p (a e)"), loc1[384:512, :], tk[4], reads=[R_loc1], writes=[V])
                kb.dma("sync", abias[:], abias_d[:, :], tk[5], writes=[abias])
                kb.dma("sync", dtab[:].rearrange("p r q -> p (r q)"), dtab_d[:, :], tk[6], writes=[dtab])
                for vi in range(2):
                    for m in range(2):
                        for b in range(2):
                            kb.dma("gpsimd", QS[vi][m][b][64:66, :], qaug_d[2 * vi:2 * vi + 2, :], tqs[vi][m][b], pwrites=[QS[vi][m][b]])

                if i == 0:
                    for nm_, j_ in needed_weights():
                        weight(nm_, j_)
                items = [(qb, kbk) for qb in range(32) for kbk in range(128)]

                def load_q(qb):
                    b = qb % 2
                    for vi in range(2):
                        for m in range(2):
                            q = QS[vi][m][b]
                            kb.dma("sync", q[0:64, :], loc1[64 * m:64 * (m + 1), qb * 512:(qb + 1) * 512], tqs[vi][m][b],
                                   reads=[R_loc1], pwrites=[q])

                def kind(qb, kbk):
                    if kbk < 4 * qb:
                        return 0
                    if kbk < 4 * qb + 4:
                        return 2
                    return 1

                def emit_S(idx):
                    qb, kbk = items[idx]
                    if kbk == 32 and qb + 1 < 32:
                        load_q(qb + 1)
                    kd = kind(qb, kbk)
                    sp = scp[idx % 2]
                    rows = 64 if kd == 2 else 66
                    for m in range(2):
                        q = QS[0 if kd != 1 else 1][m][qb % 2]
                        kb.op("tensor", "matmul", sp[:, m * 512:(m + 1) * 512], KT[m][0:rows, kbk * 128:(kbk + 1) * 128], q[0:rows, :],
                              start=True, stop=True, reads=[KT[m], q], writes=[sp], signal=(m == 1))
                    p = Pb[idx % 3]
                    if kd == 2:
                        r = kbk - 4 * qb
                        sc = scs[idx % 2]
                        for m in range(2):
                            kb.op("vector", "tensor_tensor", sc[:, m * 512:(m + 1) * 512], sp[:, m * 512:(m + 1) * 512], dtab[:, r, :], ALU.add,
                                  reads=[sp, dtab], pwrites=[sc])
                        kb.op("scalar", "activation", p[:], sc[:], AF.Exp, reads=[sc], writes=[p])
                    else:
                        if kd == 0:
                            n = 4 * qb - kbk
                            col = n
                        else:
                            n = kbk - 4 * qb
                            col = 129 + n
                        kb.op("scalar", "activation", p[:], sp[:], AF.Exp, bias=abias[:, col:col + 1], reads=[sp, abias], writes=[p])

                def emit_AV(idx):
                    qb, kbk = items[idx]
                    p = Pb[idx % 3]
                    for m in range(2):
                        kb.op("tensor", "matmul", Op[m][:], V[:, kbk, :], p[:, m * 512:(m + 1) * 512], start=(kbk == 0), stop=(kbk == 127),
                              reads=[V, p], writes=[Op[m]], signal=(m == 1))
                    aA = accA[qb % 2]
                    if kbk == 0:
                        kb.op("vector", "tensor_copy", aA[:], p[:], reads=[p], writes=[aA])
                    else:
                        kb.op("vector", "tensor_tensor", aA[:], aA[:], p[:], ALU.add, reads=[p, aA], writes=[aA])
                    if kbk == 127:
                        kb.op("vector", "tensor_copy", acch[:], aA[:], reads=[aA], writes=[acch])
                        kb.op("vector", "tensor_tensor", accl[:], aA[:], acch[:], ALU.subtract, reads=[aA, acch], writes=[accl])
                        for m in range(2):
                            kb.op("tensor", "matmul", Lp[m][:], ones[:], acch[:, m * 512:(m + 1) * 512], start=True, stop=False,
                                  reads=[ones, acch], writes=[Lp[m]], signal=False)
                            kb.op("tensor", "matmul", Lp[m][:], ones[:], accl[:, m * 512:(m + 1) * 512], start=False, stop=True,
                                  reads=[ones, accl], writes=[Lp[m]], signal=True)
                        post(qb)

                def post(qb):
                    kb.op("vector", "reciprocal", r1[:], Lp[0][:], reads=[Lp[0]], writes=[r1])
                    kb.op("vector", "reciprocal", r2[:], Lp[1][:], reads=[Lp[1]], writes=[r2])
                    kb.op("vector", "tensor_tensor", o1[:], Op[0][:], r1[:], ALU.mult, reads=[Op[0], r1], writes=[o1])
                    kb.op("vector", "tensor_tensor", o2[:], Op[1][:], r2[:], ALU.mult, reads=[Op[1], r2], writes=[o2])
                    kb.op("vector", "scalar_tensor_tensor", od[:], o2[:], lamt[:, 2 * j:2 * j + 1], o1[:], ALU.mult, ALU.add,
                          reads=[o2, lamt, o1], writes=[od])
                    kb.op("scalar", "activation", osq[:], od[:], AF.Square, reads=[od], writes=[osq])
                    kb.op("tensor", "matmul", Lp[0][:], ones[:], osq[:], start=True, stop=True, reads=[ones, osq], writes=[Lp[0]])
                    kb.op("scalar", "activation", rr[:], Lp[0][:], AF.Sqrt, bias=EPS, scale=1.0 / 128, reads=[Lp[0]], writes=[rr])
                    kb.op("vector", "reciprocal", rr[:], rr[:], reads=[rr], writes=[rr])
                    o = on[qb % 2]
                    kb.op("vector", "scalar_tensor_tensor", o[:], od[:], lamt[:, 2 * j + 1:2 * j + 2], rr[:], ALU.mult, ALU.mult,
                          reads=[od, lamt, rr], writes=[o])
                    kb.dma("sync", s2v[1][:, qb * 512:(qb + 1) * 512], o[:], ton[qb % 2], reads=[o], pwrites=[R_send2])

                load_q(0)
                emit_S(0)
                for idx in range(len(items)):
                    if idx + 1 < len(items):
                        emit_S(idx + 1)
                    emit_AV(idx)
                kb.flush()

            if stop == "A2b":
                return True
            kb.allgather(send2.ap().opt(), recv2.ap().opt(), reads=[R_send2], writes=[R_recv2])
            r2v = recv2.ap().rearrange("(jt p) (r c) -> r p jt c", jt=16, r=8)

            with ExitStack() as es:
                yt = sb(es, "yt", [128, 16, T], BF16)
                sg = [sb(es, "sg%d" % b, [128, T], BF16) for b in range(2)]
                wpl = sb(es, "wpl", [128, 8, 256], BF16)
                wo = [sb(es, "wo%d" % b, [128, KC, 256], BF16) for b in range(2)]
                xin = [sb(es, "xin%d" % b, [128, T], F32) for b in range(2)]
                xo = [sb(es, "xo%d" % b, [128, T], F32) for b in range(2)]
                pp = [ps(es, "pp%d" % b, [128, T]) for b in range(2)]
                ty = [kb.dtrk() for _ in range(2)]
                tsg = [kb.dtrk() for _ in range(2)]
                twp = kb.dtrk()
                two = [kb.dtrk() for _ in range(2)]
                txi = [kb.dtrk() for _ in range(2)]
                txo = [kb.dtrk() for _ in range(2)]
                kb.dma("gpsimd", yt[:], lambda e, c: r2v[_rank(e, c)], ty[0], reads=[R_recv2], writes=[yt])
                wfd, R_w = weight("w_pool", j)
                kb.dma("gpsimd", wpl[:], wfd.ap().rearrange("(g k p) d -> p (g k) d", g=4, p=128), twp, reads=[R_w], writes=[wpl])
                for blk in range(8):
                    g = blk // 2
                    dh = blk % 2
                    pz = pp[blk % 2]
                    s = sg[blk % 2]
                    kb.dma("sync", s[:], gbuf[blk * 128:(blk + 1) * 128, :], tsg[blk % 2], reads=[R_gbuf], writes=[s])
                    for kk in range(2):
                        for t in range(4):
                            kb.op("tensor", "matmul", pz[:, t * 512:(t + 1) * 512], wpl[:, 2 * g + kk, dh * 128:(dh + 1) * 128],
                                  yt[:, 2 * (2 * g + kk), t * 512:(t + 1) * 512], start=(kk == 0), stop=(kk == 1), reads=[wpl, yt], writes=[pz],
                                  signal=(kk == 1 and t == 3))
                    if dh == 0:
                        kb.op("vector", "scalar_tensor_tensor", s[:], pz[:], pscale[:, j * 8 + blk:j * 8 + blk + 1], s[:], ALU.mult, ALU.mult,
                              reads=[pz, pscale, s], writes=[s])
                        keep = s
                    else:
                        kb.op("vector", "scalar_tensor_tensor", yt[:, 2 * blk, :], pz[:], pscale[:, j * 8 + blk:j * 8 + blk + 1], s[:], ALU.mult, ALU.mult,
                              reads=[pz, pscale, s], pwrites=[yt])
                        kb.op("vector", "tensor_copy", yt[:, 2 * (blk - 1), :], keep[:], reads=[keep], pwrites=[yt])
                for hd in range(8):
                    blk = 8 + hd
                    s = sg[blk % 2]
                    kb.dma("sync", s[:], gbuf[blk * 128:(blk + 1) * 128, :], tsg[blk % 2], reads=[R_gbuf], writes=[s])
                    kb.op("vector", "tensor_tensor", yt[:, 2 * hd + 1, :], yt[:, 2 * hd + 1, :], s[:], ALU.mult, reads=[yt, s], pwrites=[yt])
                wfd, R_w = weight("w_out_ab", j)
                wsrc = wfd.ap().rearrange("(k p) c -> p k c", p=128)
                for wp in range(8):
                    w = wo[wp % 2]
                    kb.dma("gpsimd", w[:], wsrc[:, :, wp * 256:(wp + 1) * 256], two[wp % 2], reads=[R_w], writes=[w])
                    for c2 in range(2):
                        db = 2 * wp + c2
                        pz = pp[db % 2]
                        xi = xin[db % 2]
                        x_o = xo[db % 2]
                        kb.dma("sync", xi[:], xsrc[db * 128:(db + 1) * 128, :], txi[db % 2], reads=[R_src], writes=[xi])
                        for k in range(KC):
                            for t in range(4):
                                kb.op("tensor", "matmul", pz[:, t * 512:(t + 1) * 512], w[:, k, c2 * 128:(c2 + 1) * 128],
                                      yt[:, 2 * (k % 8) + (k // 8), t * 512:(t + 1) * 512], start=(k == 0), stop=(k == KC - 1), reads=[w, yt], writes=[pz],
                                      signal=(k == KC - 1 and t == 3))
                        kb.op("vector", "scalar_tensor_tensor", x_o[:], pz[:], mcol(i, 32 + db), xi[:], ALU.mult, ALU.add,
                              reads=[pz, modT, xi], writes=[x_o])
                        kb.dma("sync", xdst[db * 128:(db + 1) * 128, :], x_o[:], txo[db % 2], reads=[x_o], pwrites=[R_dst])
                kb.flush()


        def layer_c(i, xsrc, R_src, xdst, R_dst):
            j = i // 2
            s3v = send3.ap().rearrange("(j r m) c -> j r m c", j=8, r=2)
            with ExitStack() as outer:
                uT = sb(outer, "uT", [128, KC, T], BF16)
                with ExitStack() as mid:
                    hT = sb(mid, "hT", [128, KC, T], BF16)
                    stage_in(i, xsrc, R_src, hT)
                    with ExitStack() as es:
                        wb = [sb(es, "wb%d" % b, [128, KC, 256], BF16) for b in range(2)]
                        ob = [sb(es, "ob%d" % b, [128, 1024], BF16) for b in range(3)]
                        zps = [ps(es, "zps%d" % b, [128, 1024]) for b in range(2)]
                        twb = [kb.dtrk() for _ in range(2)]
                        tob = [kb.dtrk() for _ in range(3)]
                        wfd, R_w = weight("w_in_c", j)
                        wsrc = wfd.ap().rearrange("(k p) c -> p k c", p=128)
                        zc = 0
                        oc = 0
                        for wp in range(16):
                            w = wb[wp % 2]
                            kb.dma("gpsimd", w[:], wsrc[:, :, wp * 256:(wp + 1) * 256], twb[wp % 2], reads=[R_w], writes=[w])
                            for c2 in range(2):
                                cb = 2 * wp + c2
                                for th in range(2):
                                    zp = zps[zc % 2]
                                    zc += 1
                                    for k in range(KC):
                                        for t in range(2):
                                            kb.op("tensor", "matmul", zp[:, t * 512:(t + 1) * 512], w[:, k, c2 * 128:(c2 + 1) * 128],
                                                  hT[:, k, th * 1024 + t * 512: th * 1024 + (t + 1) * 512],
                                                  start=(k == 0), stop=(k == KC - 1), reads=[hT, w], writes=[zp],
                                                  signal=(k == KC - 1 and t == 1))
                                    if cb < 16:
                                        kb.op("scalar", "activation", uT[:, cb, th * 1024:(th + 1) * 1024], zp[:], AF.Copy, reads=[zp], pwrites=[uT])
                                    else:
                                        o = ob[oc % 3]
                                        to = tob[oc % 3]
                                        oc += 1
                                        kb.op("scalar", "activation", o[:], zp[:], AF.Silu, reads=[zp], writes=[o])
                                        gb = cb - 16
                                        kb.dma("sync", gbuf[gb * 128:(gb + 1) * 128, th * 1024:(th + 1) * 1024], o[:], to, reads=[o], pwrites=[R_gbuf])
                        kb.flush()
                with ExitStack() as es:
                    cs = sb(es, "cs", [128, 4, 1024], BF16)
                    oa = [sb(es, "oa%d" % b, [128, T], BF16) for b in range(3)]
                    aps = [ps(es, "aps%d" % b, [128, T]) for b in range(2)]
                    tcs = kb.dtrk()
                    toa = [kb.dtrk() for _ in range(3)]
                    kb.dma("gpsimd", cs[:], cs512_d.ap().rearrange("(k p) c -> p k c", p=128), tcs, writes=[cs])
                    cnt = 0
                    for g in range(4):
                        for mb in range(8):
                            pz = aps[cnt % 2]
                            o = oa[cnt % 3]
                            to = toa[cnt % 3]
                            cnt += 1
                            for cc in range(4):
                                for t in range(4):
                                    kb.op("tensor", "matmul", pz[:, t * 512:(t + 1) * 512], cs[:, cc, mb * 128:(mb + 1) * 128],
                                          uT[:, 4 * g + cc, t * 512:(t + 1) * 512], start=(cc == 0), stop=(cc == 3), reads=[cs, uT], writes=[pz],
                                          signal=(cc == 3 and t == 3))
                            kb.op("scalar", "activation", o[:], pz[:], AF.Copy, reads=[pz], writes=[o])
                            ri = mb // 4
                            mglob = 512 * g + (mb % 4) * 128
                            dj = mglob // 256
                            ml = mglob % 256
                            kb.dma("sync", s3v[dj, ri][ml:ml + 128, :], o[:], to, reads=[o], pwrites=[R_send3])
                    kb.flush()

            if stop == "C1":
                return True
            kb.allgather(send3.ap().opt(), recv3.ap().opt(), reads=[R_send3], writes=[R_recv3])
            r3v = recv3.ap().rearrange("(s j q) c -> j s (q c)", s=8, j=8)

            with ExitStack() as es:
                U = [[sb(es, "U%d%d" % (r, b), [128, 64, 128], BF16) for b in range(2)] for r in range(2)]
                F1 = sb(es, "F1", [128, 256], BF16)
                F2 = sb(es, "F2", [128, 256], BF16)
                TW = sb(es, "TW", [128, 1024], F32)
                Ysb = [sb(es, "Ysb%d" % b, [128, 1024], F32) for b in range(2)]
                tt_ = [[sb(es, "tw%d%d" % (a, b), [128, 512], F32) for b in range(2)] for a in range(4)]
                Ypr = [sb(es, "Ypr%d" % b, [128, 512], BF16) for b in range(2)]
                Ypi = [sb(es, "Ypi%d" % b, [128, 512], BF16) for b in range(2)]
                fo = [sb(es, "fo%d" % b, [128, 64, 128], BF16) for b in range(2)]
                yps = [ps(es, "yps%d" % b, [128, 1024]) for b in range(2)]
                xps = [ps(es, "xps%d" % b, [128, 512]) for b in range(2)]
                tu = [[[kb.dtrk() for s_ in range(2)] for b in range(2)] for r in range(2)]
                tf = [kb.dtrk() for _ in range(3)]
                tfo = [kb.dtrk() for _ in range(2)]
                kb.dma("gpsimd", F1[:], f1_d[:, :], tf[0], writes=[F1])
                kb.dma("gpsimd", F2[:], f2_d[:, :], tf[1], writes=[F2])
                kb.dma("sync", TW[:], tw_d[:, :], tf[2], writes=[TW])
                Tc4 = TW[:, 0:512].rearrange("p (m k) -> p m k", m=4)
                Ts4 = TW[:, 512:1024].rearrange("p (m k) -> p m k", m=4)
                norm = 1.0 / math.sqrt(S * 512.0)
                cnt = 0
                kb.dma("gpsimd", loc3.ap().rearrange("(s q) c -> s (q c)", s=8), lambda e, c: r3v[_rank(e, c)], kb.dtrk(),
                       reads=[R_recv3], writes=[R_loc3])
                l3v = loc3.ap().rearrange("(s r m) c -> s r m c", r=2, s=8)
                for qt in range(4):
                    b = qt % 2
                    for r in range(2):
                        for s_ in range(8):
                            kb.dma("sync", U[r][b][16 * s_:16 * (s_ + 1), :, :],
                                   l3v[s_][r][64 * qt:64 * (qt + 1)].rearrange("m (l n) -> l m n", l=16),
                                   tu[r][b][s_ % 2], reads=[R_loc3], pwrites=[U[r][b]])
                    for ch in range(16):
                        yp = yps[cnt % 2]
                        ys = Ysb[cnt % 2]
                        for mi in range(4):
                            m_ = 4 * ch + mi
                            kb.op("tensor", "matmul", yp[:, mi * 256:(mi + 1) * 256], U[0][b][:, m_, :], F1[:], start=True, stop=False,
                                  reads=[U[0][b], F1], writes=[yp], signal=False)
                            kb.op("tensor", "matmul", yp[:, mi * 256:(mi + 1) * 256], U[1][b][:, m_, :], F2[:], start=False, stop=True,
                                  reads=[U[1][b], F2], writes=[yp], signal=(mi == 3))
                        kb.op("scalar", "activation", ys[:], yp[:], AF.Copy, reads=[yp], writes=[ys])
                        yv = ys[:].rearrange("p (m r k) -> p m r k", m=4, r=2)
                        Yr = yv[:, :, 0, :]
                        Yi = yv[:, :, 1, :]
                        t1, t2, t3, t4 = [tt_[a][cnt % 2] for a in range(4)]
                        v3 = lambda t: t[:].rearrange("p (m k) -> p m k", m=4)
                        kb.op("vector", "tensor_tensor", v3(t1), Yr, Tc4, ALU.mult, reads=[ys, TW], writes=[t1])
                        kb.op("gpsimd", "tensor_tensor", v3(t2), Yi, Ts4, ALU.mult, reads=[ys, TW], writes=[t2])
                        kb.op("vector", "tensor_tensor", v3(t3), Yi, Tc4, ALU.mult, reads=[ys, TW], writes=[t3])
                        kb.op("gpsimd", "tensor_tensor", v3(t4), Yr, Ts4, ALU.mult, reads=[ys, TW], writes=[t4])
                        pr = Ypr[cnt % 2]
                        pi = Ypi[cnt % 2]
                        kb.op("vector", "tensor_tensor", pr[:], t1[:], t2[:], ALU.add, reads=[t1, t2], writes=[pr])
                        kb.op("vector", "tensor_tensor", pi[:], t3[:], t4[:], ALU.subtract, reads=[t3, t4], writes=[pi])
                        xp = xps[cnt % 2]
                        kb.op("tensor", "matmul", xp[:], F2[:, 128:256], pr[:], start=True, stop=False, reads=[F2, pr], writes=[xp], signal=False)
                        kb.op("tensor", "matmul", xp[:], F2[:, 0:128], pi[:], start=False, stop=True, reads=[F2, pi], writes=[xp])
                        kb.op("scalar", "activation", fo[b][:, 4 * ch:4 * ch + 4, :].rearrange("p m k -> p (m k)"), xp[:], AF.Copy, scale=norm,
                              reads=[xp], pwrites=[fo[b]])
                        cnt += 1
                    kb.dma("sync", send4[:, qt * 8192:(qt + 1) * 8192], fo[b][:].rearrange("p m k -> p (m k)"), tfo[b], reads=[fo[b]], pwrites=[R_send4])
                kb.flush()

            if stop == "C2":
                return True
            kb.allgather(send4.ap().opt(), recv4.ap().opt(), reads=[R_send4], writes=[R_recv4])
            r4v = recv4.ap().rearrange("r (a b) -> (r a) b", a=2).rearrange("(j q la) b -> q j la b", j=8, q=8)

            with ExitStack() as es:
                yt = sb(es, "yt", [128, 16, T], BF16)
                ftg = [sb(es, "ftg%d" % b, [128, 4, T], BF16) for b in range(2)]
                wf = [sb(es, "wf%d" % b, [128, 4, 512], BF16) for b in range(2)]
                sg = [sb(es, "sg%d" % b, [128, T], BF16) for b in range(2)]
                wo = [sb(es, "wo%d" % b, [128, KC, 256], BF16) for b in range(2)]
                xin = [sb(es, "xin%d" % b, [128, T], F32) for b in range(2)]
                xo = [sb(es, "xo%d" % b, [128, T], F32) for b in range(2)]
                pp = [ps(es, "pp%d" % b, [128, T]) for b in range(2)]
                tft = [[kb.dtrk() for _ in range(4)] for b in range(2)]
                twf = [kb.dtrk() for _ in range(2)]
                tsg = [kb.dtrk() for _ in range(2)]
                two = [kb.dtrk() for _ in range(2)]
                txi = [kb.dtrk() for _ in range(2)]
                txo = [kb.dtrk() for _ in range(2)]
                kb.dma("gpsimd", loc4.ap().rearrange("r (a b) -> (r a) b", a=2).rearrange("(j la) b -> j la b", j=8), lambda e, c: r4v[_rank(e, c)], kb.dtrk(),
                       reads=[R_recv4], writes=[R_loc4])
                l4v = loc4.ap().rearrange("(j l) c -> j l c", j=8)
                wffd, R_wf = weight("w_fourier", j)
                wfsrc = wffd.ap().rearrange("(g k p) d -> g p k d", g=4, p=128)
                cnt = 0
                for g in range(4):
                    ft = ftg[g % 2]
                    for mc in range(4):
                        blk = 4 * g + mc
                        kb.dma("sync", ft[:, mc, :].rearrange("p (l k) -> p l k", l=16),
                               l4v[blk // 2].rearrange("l (h m k) -> h m l k", h=2, m=128)[blk % 2], tft[g % 2][mc], reads=[R_loc4], pwrites=[ft])
                    kb.dma("gpsimd", wf[g % 2][:], wfsrc[g], twf[g % 2], reads=[R_wf], writes=[wf[g % 2]])
                    for db in range(4):
                        dblk = 4 * g + db
                        pz = pp[cnt % 2]
                        s_ = sg[cnt % 2]
                        kb.dma("sync", s_[:], gbuf[dblk * 128:(dblk + 1) * 128, :], tsg[cnt % 2], reads=[R_gbuf], writes=[s_])
                        cnt += 1
                        for mc in range(4):
                            for t in range(4):
                                kb.op("tensor", "matmul", pz[:, t * 512:(t + 1) * 512], wf[g % 2][:, mc, db * 128:(db + 1) * 128],
                                      ft[:, mc, t * 512:(t + 1) * 512], start=(mc == 0), stop=(mc == 3), reads=[wf[g % 2], ft], writes=[pz],
                                      signal=(mc == 3 and t == 3))
                        kb.op("vector", "tensor_tensor", yt[:, dblk, :], pz[:], s_[:], ALU.mult, reads=[pz, s_], pwrites=[yt])
                wfd, R_w = weight("w_out_c", j)
                wsrc = wfd.ap().rearrange("(k p) c -> p k c", p=128)
                for wp in range(8):
                    w = wo[wp % 2]
                    kb.dma("gpsimd", w[:], wsrc[:, :, wp * 256:(wp + 1) * 256], two[wp % 2], reads=[R_w], writes=[w])
                    for c2 in range(2):
                        db = 2 * wp + c2
                        pz = pp[db % 2]
                        xi = xin[db % 2]
                        x_o = xo[db % 2]
                        kb.dma("sync", xi[:], xsrc[db * 128:(db + 1) * 128, :], txi[db % 2], reads=[R_src], writes=[xi])
                        for k in range(KC):
                            for t in range(4):
                                kb.op("tensor", "matmul", pz[:, t * 512:(t + 1) * 512], w[:, k, c2 * 128:(c2 + 1) * 128],
                                      yt[:, k, t * 512:(t + 1) * 512], start=(k == 0), stop=(k == KC - 1), reads=[w, yt], writes=[pz],
                                      signal=(k == KC - 1 and t == 3))
                        kb.op("vector", "scalar_tensor_tensor", x_o[:], pz[:], mcol(i, 32 + db), xi[:], ALU.mult, ALU.add,
                              reads=[pz, modT, xi], writes=[x_o])
                        kb.dma("sync", xdst[db * 128:(db + 1) * 128, :], x_o[:], txo[db % 2], reads=[x_o], pwrites=[R_dst])
                kb.flush()

        srcs = [(xT, R_xT)]
        bufs = [(xa, R_xa), (xb, R_xb)]
        cur = (xT, R_xT)
        for i in range(NL):
            dst = (yT, R_y) if i == NL - 1 else bufs[i % 2]
            if i % 2 == 0:
                if layer_ab(i, cur[0], cur[1], dst[0], dst[1]):
                    break
            else:
                if layer_c(i, cur[0], cur[1], dst[0], dst[1]):
                    break
            cur = dst
        if debug:
            dl = [("send1", send1, R_send1), ("send2", send2, R_send2), ("gbuf", gbuf, R_gbuf)]
            if NL > 1:
                dl += [("send3", send3, R_send3), ("send4", send4, R_send4)]
            for nm, src, Rs in dl:
                if nm in debug:
                    shp = list(src.shape)
                    dbg = nc.dram_tensor("dbg_" + nm, shp, BF16, kind="ExternalOutput")
                    kb.dma("sync", dbg.ap().rearrange("(a p) c -> p a c", p=128), src.ap().rearrange("(a p) c -> p a c", p=128), kb.dtrk(), reads=[Rs])
            kb.flush()
    return nc, used_inputs


def _host_inputs(inp, r):
    f = np.float32
    x = np.asarray(inp["x"], f)[0]
    m = {}
    m["xT"] = np.ascontiguousarray(x[r * T:(r + 1) * T, :].T)
    c = np.asarray(inp["c"], f)[0]
    m["cvec"] = np.ascontiguousarray(c.reshape(16, 128).T)
    nw = np.asarray(inp["norm_w"], f)
    m["nw"] = np.ascontiguousarray(nw.reshape(4, 16, 128).transpose(2, 0, 1).reshape(128, 64))
    m["adaw"] = np.ascontiguousarray(np.asarray(inp["ada_w"], f)[:, :, r * 768:(r + 1) * 768])
    ab = np.asarray(inp["ada_b"], f)[:, r * 768:(r + 1) * 768]
    m["adab"] = np.ascontiguousarray(ab.reshape(4, 6, 128).transpose(2, 0, 1).reshape(128, 24))
    for k in ["w_in_ab", "w_pool", "w_out_ab", "w_in_c", "w_fourier", "w_out_c"]:
        wa = np.asarray(inp[k], f)
        for jj in range(2):
            w2 = wa[jj].reshape(-1, wa.shape[-1])
            n = w2.shape[0] // NCORES
            m["%s_%d" % (k, jj)] = np.ascontiguousarray(w2[r * n:(r + 1) * n])
    psc = np.asarray(inp["pool_scale"], f)
    m["pscale"] = np.ascontiguousarray(psc.reshape(2, 8, 128).transpose(2, 0, 1).reshape(128, 16))
    qn = np.asarray(inp["q_norm_w"], f)
    kn = np.asarray(inp["k_norm_w"], f)
    qkw = np.zeros((128, 4), f)
    for j in range(2):
        qkw[:, 2 * j] = np.tile(qn[j], 2)
        qkw[:, 2 * j + 1] = np.tile(kn[j], 2)
    m["qkw"] = qkw
    lam = np.stack([np.asarray(inp[k], f) for k in ["lambda_q1", "lambda_k1", "lambda_q2", "lambda_k2"]], axis=1)
    m["lamv"] = np.ascontiguousarray(np.broadcast_to(lam.reshape(1, 512), (128, 512)))
    m["subw"] = np.ascontiguousarray(np.asarray(inp["subln_w"], f).T)
    slope = 2.0 ** (-(r + 1))
    t = np.arange(S)
    m["kaug"] = np.stack([np.ones(S), slope * (t % 128)]).astype(f)
    q2 = (np.arange(512) - 256).astype(np.float64)
    m["qaug"] = np.stack([-slope * q2, np.ones(512), slope * q2, -np.ones(512)]).astype(f)
    ab_ = np.zeros(257)
    for n in range(1, 129):
        ab_[n] = -slope * (128 * n + 256)
    for n in range(4, 128):
        ab_[129 + n] = -slope * (128 * n - 256)
    m["abias"] = np.ascontiguousarray(np.broadcast_to(ab_.astype(f), (128, 257)))
    kk = np.arange(128)[:, None, None]
    rr = np.arange(4)[None, :, None]
    qq = np.arange(512)[None, None, :]
    m["dtab"] = (-slope * np.abs(qq - (128 * rr + kk))).astype(f).reshape(128, 2048)
    wins = (2, 4, 8, 16)
    w = wins[r // 2]
    pc = np.array([1.0 / ww if ww == w else 0.0 for ww in wins], f)
    m["pcoef"] = np.ascontiguousarray(np.broadcast_to(pc, (128, 4)))
    te = np.concatenate([np.arange(8), np.arange(S - 8, S)])
    lo = np.clip(te - w // 2, 0, S - 1)
    hi = np.clip(te + (w - w // 2) - 1, 0, S - 1)
    ratio = (w / (hi - lo + 1)).astype(f)
    m["pratio"] = np.ascontiguousarray(np.broadcast_to(ratio, (128, 16)))
    cc = np.arange(512)[:, None] * np.arange(512)[None, :]
    ang = 2.0 * np.pi * (cc % 512) / 512.0
    m["cs512"] = np.concatenate([np.cos(ang), -np.sin(ang)], axis=1).astype(f)
    aa = np.arange(128)[:, None] * np.arange(128)[None, :]
    a128 = 2.0 * np.pi * (aa % 128) / 128.0
    Fc, Fs = np.cos(a128), np.sin(a128)
    m["f1c"] = np.concatenate([Fc, -Fs], axis=1).astype(f)
    m["f2c"] = np.concatenate([Fs, Fc], axis=1).astype(f)
    at = 2.0 * np.pi * aa / float(S)
    m["twc"] = np.concatenate([np.tile(np.cos(at), (1, 4)), np.tile(np.sin(at), (1, 4))], axis=1).astype(f)
    return m


_NC_CACHE = {}


def kernel(**inputs):
    NL = 4
    if NL not in _NC_CACHE:
        _NC_CACHE[NL] = build(NL)
    nc, used = _NC_CACHE[NL]
    in_maps = []
    for r in range(NCORES):
        hm = _host_inputs(inputs, r)
        in_maps.append({k: hm[k] for k in used})
    res = run_bass_kernel_spmd(nc, in_maps, core_ids=list(range(NCORES)))
    out = np.concatenate([np.asarray(res.results[r]["yT"]).T for r in range(NCORES)], axis=0)
    return out[None].astype(np.float32)
```
